# Optimizing a Trainium2 kernel written in Bass

```python
import math
import jax, jax.numpy as jnp
from jax import lax
import numpy as np

D_MODEL = 2048
BATCH = 4
SEQ = 2048
DEPTH = 2

ROPE_THETA = 10000.0
ROPE_DIM = 64
Q_BLOCK = 128
NORM_EPS = 1e-6
NEG_INF = -1e30

DA_HEADS = 4
DA_QK_DIM = 64
DA_V_DIM = 128
DA_WIDTH = DA_HEADS * DA_V_DIM
DA_SUBLN_EPS = 1e-5

SW_HEADS = 16
SW_KV_HEADS = 2
SW_HEAD_DIM = 64
SW_WINDOW = 128
SW_WIDTH = SW_HEADS * SW_HEAD_DIM
SW_KV_WIDTH = SW_KV_HEADS * SW_HEAD_DIM

MLA_HEADS = 4
MLA_NOPE_DIM = 128
MLA_ROPE_DIM = 64
MLA_V_DIM = 128
MLA_Q_RANK = 512
MLA_KV_RANK = 512
MLA_WIDTH = MLA_HEADS * MLA_V_DIM

D_MIX = DA_WIDTH + SW_WIDTH + MLA_WIDTH

IN_SPLITS = (
    2 * DA_HEADS * DA_QK_DIM, 2 * DA_HEADS * DA_QK_DIM, DA_WIDTH, DA_WIDTH,
    SW_WIDTH, SW_KV_WIDTH, SW_KV_WIDTH, SW_WIDTH,
    MLA_Q_RANK, MLA_KV_RANK, MLA_ROPE_DIM, MLA_WIDTH,
)
D_IN = 5952

kernel_name = "hybrid_diffattn_swa_sink_mla_parallel_heads"


def rms_norm(x, w, eps=NORM_EPS):
    xf = x.astype(jnp.float32)
    xf = xf * lax.rsqrt(jnp.mean(xf * xf, axis=-1, keepdims=True) + eps)
    return xf.astype(x.dtype) * w


def rope_tables(positions, dim):
    inv_freq = jnp.power(ROPE_THETA, -jnp.arange(0, dim, 2, dtype=jnp.float32) / dim)
    ang = positions.astype(jnp.float32)[..., None] * inv_freq
    return jnp.cos(ang), jnp.sin(ang)


def apply_rope(x, cos, sin):
    half = x.shape[-1] // 2
    x1, x2 = x[..., :half], x[..., half:]
    c, s = cos.astype(x.dtype), sin.astype(x.dtype)
    return jnp.concatenate([x1 * c - x2 * s, x2 * c + x1 * s], axis=-1)


def to_query_blocks(q):
    B, S = q.shape[:2]
    nb = S // Q_BLOCK
    qb = q.reshape((B, nb, Q_BLOCK) + q.shape[2:])
    return jnp.moveaxis(qb, 1, 0), nb


def from_query_blocks(o):
    o = jnp.moveaxis(o, 0, 1)
    return o.reshape((o.shape[0], o.shape[1] * o.shape[2]) + o.shape[3:])


def diff_attention(q, k, v, lam, subln_w, lam_init):
    S = q.shape[1]
    scale = q.shape[-1] ** -0.5
    qb, nb = to_query_blocks(q)
    k_pos = jnp.arange(S)

    def block(args):
        qblk, i = args
        s = jnp.einsum('bqhcd,bkhcd->bhcqk', qblk, k).astype(jnp.float32) * scale
        q_pos = i * Q_BLOCK + jnp.arange(Q_BLOCK)
        s = jnp.where(k_pos[None, :] <= q_pos[:, None], s, NEG_INF)
        p = jax.nn.softmax(s, axis=-1)
        a = (p[:, :, 0] - lam * p[:, :, 1]).astype(v.dtype)
        return jnp.einsum('bhqk,bkhd->bqhd', a, v)

    o = from_query_blocks(lax.map(block, (qb, jnp.arange(nb))))
    o = rms_norm(o, subln_w, DA_SUBLN_EPS) * (1.0 - lam_init)
    return o.reshape(o.shape[0], o.shape[1], -1)


def causal_attention(q, k, v, scale):
    S = q.shape[1]
    qb, nb = to_query_blocks(q)
    k_pos = jnp.arange(S)

    def block(args):
        qblk, i = args
        s = jnp.einsum('bqhd,bkhd->bhqk', qblk, k).astype(jnp.float32) * scale
        q_pos = i * Q_BLOCK + jnp.arange(Q_BLOCK)
        s = jnp.where(k_pos[None, :] <= q_pos[:, None], s, NEG_INF)
        p = jax.nn.softmax(s, axis=-1).astype(v.dtype)
        return jnp.einsum('bhqk,bkhd->bqhd', p, v)

    o = from_query_blocks(lax.map(block, (qb, jnp.arange(nb))))
    return o.reshape(o.shape[0], o.shape[1], -1)


def sliding_window_attention(q, k, v, sinks):
    B, S, Hq, d = q.shape
    Hkv = k.shape[2]
    rep = Hq // Hkv
    W = SW_WINDOW
    nb = S // W
    qb = q.reshape(B, nb, W, Hkv, rep, d)

    def band(t):
        tb = t.reshape(B, nb, W, Hkv, d)
        prev = jnp.concatenate([jnp.zeros_like(tb[:, :1]), tb[:, :-1]], axis=1)
        return jnp.concatenate([prev, tb], axis=2)

    kb, vb = band(k), band(v)
    s = jnp.einsum('bnqgrd,bnkgd->bngrqk', qb, kb).astype(jnp.float32) * (d ** -0.5)
    qi = jnp.arange(W)[:, None]
    kj = jnp.arange(2 * W)[None, :]
    rel = qi + W - kj
    in_band = (rel >= 0) & (rel < W)
    has_prev = (jnp.arange(nb) > 0)[:, None, None] | (kj >= W)[None]
    mask = in_band[None] & has_prev
    s = jnp.where(mask[None, :, None, None], s, NEG_INF)
    sink = jnp.broadcast_to(sinks.astype(jnp.float32).reshape(1, 1, Hkv, rep, 1, 1),
                            s.shape[:-1] + (1,))
    p = jax.nn.softmax(jnp.concatenate([s, sink], axis=-1), axis=-1)[..., :-1].astype(v.dtype)
    o = jnp.einsum('bngrqk,bnkgd->bnqgrd', p, vb)
    return o.reshape(B, S, Hq * d)


def hybrid_layer(x, cos, sin, layer, pre_w, post_w, w_in, lq1, lk1, lq2, lk2, subln_w,
                 sinks, q_norm_w, kv_norm_w, w_uq, w_ukv, w_out):
    B, S, _ = x.shape
    h = rms_norm(x, pre_w)
    proj = h @ w_in
    offsets = np.cumsum(IN_SPLITS)[:-1].tolist()
    a_q, a_k, a_v, a_g, b_q, b_k, b_v, b_g, c_q, c_kv, c_kr, c_g = jnp.split(proj, offsets, axis=-1)

    cos4, sin4 = cos[:, :, None, :], sin[:, :, None, :]
    cos5, sin5 = cos4[:, :, :, None], sin4[:, :, :, None]

    qa = apply_rope(a_q.reshape(B, S, DA_HEADS, 2, DA_QK_DIM), cos5, sin5)
    ka = apply_rope(a_k.reshape(B, S, DA_HEADS, 2, DA_QK_DIM), cos5, sin5)
    va = a_v.reshape(B, S, DA_HEADS, DA_V_DIM)
    lam_init = 0.8 - 0.6 * math.exp(-0.3 * layer)
    lam = (jnp.exp(jnp.sum(lq1.astype(jnp.float32) * lk1.astype(jnp.float32)))
           - jnp.exp(jnp.sum(lq2.astype(jnp.float32) * lk2.astype(jnp.float32))) + lam_init)
    y_a = diff_attention(qa, ka, va, lam, subln_w, lam_init)

    qb = apply_rope(b_q.reshape(B, S, SW_HEADS, SW_HEAD_DIM), cos4, sin4)
    kb = apply_rope(b_k.reshape(B, S, SW_KV_HEADS, SW_HEAD_DIM), cos4, sin4)
    vb = b_v.reshape(B, S, SW_KV_HEADS, SW_HEAD_DIM)
    y_b = sliding_window_attention(qb, kb, vb, sinks)

    qc = (rms_norm(c_q, q_norm_w) @ w_uq).reshape(B, S, MLA_HEADS, MLA_NOPE_DIM + MLA_ROPE_DIM)
    q_nope, q_rope = qc[..., :MLA_NOPE_DIM], apply_rope(qc[..., MLA_NOPE_DIM:], cos4, sin4)
    kvc = (rms_norm(c_kv, kv_norm_w) @ w_ukv).reshape(B, S, MLA_HEADS, MLA_NOPE_DIM + MLA_V_DIM)
    k_nope, vc = kvc[..., :MLA_NOPE_DIM], kvc[..., MLA_NOPE_DIM:]
    k_rope = apply_rope(c_kr[:, :, None, :], cos4, sin4)
    qc = jnp.concatenate([q_nope, q_rope], axis=-1)
    kc = jnp.concatenate([k_nope, jnp.broadcast_to(k_rope, (B, S, MLA_HEADS, MLA_ROPE_DIM))], axis=-1)
    y_c = causal_attention(qc, kc, vc, (MLA_NOPE_DIM + MLA_ROPE_DIM) ** -0.5)

    y = jnp.concatenate([y_a, y_b, y_c], axis=-1) * jax.nn.silu(jnp.concatenate([a_g, b_g, c_g], axis=-1))
    out = y @ w_out
    return x + rms_norm(out, post_w)


def setup_inputs(seed: int = 0) -> dict:
    key = jax.random.key(seed)
    ks = jax.random.split(key, 16)
    f32 = jnp.float32
    L = DEPTH

    def nrm(k, shape, scale):
        return jax.random.normal(k, shape, f32) * scale

    return {
        "x": nrm(ks[0], (BATCH, SEQ, D_MODEL), 1.0),
        "positions": jnp.broadcast_to(jnp.arange(SEQ, dtype=jnp.int32), (BATCH, SEQ)),
        "pre_norm_w": 1.0 + nrm(ks[1], (L, D_MODEL), 0.02),
        "post_norm_w": 1.0 + nrm(ks[2], (L, D_MODEL), 0.02),
        "w_in": nrm(ks[3], (L, D_MODEL, D_IN), D_MODEL ** -0.5),
        "diff_lambda_q1": nrm(ks[4], (L, DA_QK_DIM), 0.1),
        "diff_lambda_k1": nrm(ks[5], (L, DA_QK_DIM), 0.1),
        "diff_lambda_q2": nrm(ks[6], (L, DA_QK_DIM), 0.1),
        "diff_lambda_k2": nrm(ks[7], (L, DA_QK_DIM), 0.1),
        "diff_subln_w": 1.0 + nrm(ks[8], (L, DA_V_DIM), 0.02),
        "sink_logits": nrm(ks[9], (L, SW_HEADS), 0.5),
        "mla_q_norm_w": 1.0 + nrm(ks[10], (L, MLA_Q_RANK), 0.02),
        "mla_kv_norm_w": 1.0 + nrm(ks[11], (L, MLA_KV_RANK), 0.02),
        "w_uq": nrm(ks[12], (L, MLA_Q_RANK, MLA_HEADS * (MLA_NOPE_DIM + MLA_ROPE_DIM)), MLA_Q_RANK ** -0.5),
        "w_ukv": nrm(ks[13], (L, MLA_KV_RANK, MLA_HEADS * (MLA_NOPE_DIM + MLA_V_DIM)), MLA_KV_RANK ** -0.5),
        "w_out": nrm(ks[14], (L, D_MIX, D_MODEL), D_MIX ** -0.5),
    }


def reference(x, positions, pre_norm_w, post_norm_w, w_in, diff_lambda_q1, diff_lambda_k1,
              diff_lambda_q2, diff_lambda_k2, diff_subln_w, sink_logits, mla_q_norm_w,
              mla_kv_norm_w, w_uq, w_ukv, w_out):
    cos, sin = rope_tables(positions, ROPE_DIM)
    for layer in range(DEPTH):
        x = hybrid_layer(x, cos, sin, layer, pre_norm_w[layer], post_norm_w[layer], w_in[layer],
                         diff_lambda_q1[layer], diff_lambda_k1[layer], diff_lambda_q2[layer],
                         diff_lambda_k2[layer], diff_subln_w[layer], sink_logits[layer],
                         mla_q_norm_w[layer], mla_kv_norm_w[layer], w_uq[layer], w_ukv[layer],
                         w_out[layer])
    return x
```

```python
import math
import os
from contextlib import ExitStack

import numpy as np
import concourse.bass as bass
import concourse.mybir as mybir
from concourse.bass_utils import run_bass_kernel_spmd

F32 = mybir.dt.float32
BF16 = mybir.dt.bfloat16
I32 = mybir.dt.int32
AF = mybir.ActivationFunctionType
ALU = mybir.AluOpType
PI = math.pi

D = 2048
S = 2048
DIN = 5952
KC = 16
T = 512
NSB = 4
DEPTH = 2
NCORES = 8
SAME_ENG_SYNC = True

C_PREW = 0
C_QNW = 32
C_KVNW = 40
C_SUBLN = 48
C_SINK = 50
C_INVF = 66
C_LAM = 67
NCST = C_LAM + 512


class Tl:
    __slots__ = ("name", "w", "r", "dsem", "dcnt", "excl")

    def __init__(self, name, excl=False):
        self.name = name
        self.excl = excl
        self.w = {}
        self.r = {}
        self.dsem = None
        self.dcnt = 0


class Ins:
    __slots__ = ("eng", "fn", "deps", "sig", "val", "sem", "isdma")

    def __init__(self, eng, fn, isdma=False):
        self.eng = eng
        self.fn = fn
        self.deps = ()
        self.sig = False
        self.val = 0
        self.sem = None
        self.isdma = isdma


class Sched:
    ENGS = ("pe", "act", "dve", "pool", "sp")

    def __init__(self, nc, es):
        self.nc = nc
        self.es = es
        self.streams = {e: [] for e in self.ENGS}
        self.esem = {e: es.enter_context(nc.semaphore("sem_" + e)) for e in ("pe", "act", "dve", "pool")}
        self.nsem = 0
        self.final = []
        import os
        self.nops = 0
        self.lo = int(os.environ.get("KLO", "1")) if os.environ.get("KLO") else 1 << 60
        self.hi = int(os.environ.get("KHI", "0"))
        self.marks = []
        self.maxops = int(os.environ.get("KMAXOPS", "100000000"))

    def _deps(self, ins, key, reads, writes):
        ex = [t for t in reads if t.excl]
        if ex:
            reads = [t for t in reads if not t.excl]
            writes = list(writes) + ex
        deps = set()
        for t in reads:
            deps.update(t.w.values())
        for t in writes:
            deps.update(t.w.values())
            deps.update(t.r.values())
        deps.discard(ins)
        ins.deps = tuple(deps)
        for t in reads:
            t.r[key] = ins
        for t in writes:
            t.w[key] = ins
            t.r = {}

    def op(self, eng, fn, reads=(), writes=()):
        self.nops += 1
        if self.lo <= self.nops <= self.hi:
            print("OP", self.nops, eng, [t.name for t in reads], [t.name for t in writes])
        if self.nops > self.maxops:
            return None
        ins = Ins(eng, fn)
        self._deps(ins, eng, reads, writes)
        self.streams[eng].append(ins)
        return ins

    def dma(self, q, fn, sb, reads=(), writes=(), final=False):
        self.nops += 1
        if self.nops > self.maxops:
            return None
        ins = Ins(q, fn, isdma=True)
        if sb.dsem is None:
            sb.dsem = self.es.enter_context(self.nc.semaphore("ds%d" % self.nsem))
            self.nsem += 1
        self._deps(ins, ("d", id(sb)), reads, writes)
        sb.dcnt += 16
        ins.sem = sb.dsem
        ins.val = sb.dcnt
        self.streams[q].append(ins)
        if final:
            self.final.append(ins)
        return ins

    def finalize(self):
        for st in self.streams.values():
            for ins in st:
                for d in ins.deps:
                    d.sig = True
        for e, st in self.streams.items():
            cnt = 0
            for ins in st:
                if ins.isdma:
                    continue
                if ins.sig:
                    cnt += 1
                    ins.val = cnt
                    ins.sem = self.esem[e]

    def emit(self, e, eng):
        waited = {}
        for ins in self.streams[e]:
            need = {}
            for d in ins.deps:
                if (not d.isdma) and d.eng == e and (e == "pe" or not SAME_ENG_SYNC):
                    continue
                k = id(d.sem)
                if k not in need or need[k][1] < d.val:
                    need[k] = (d.sem, d.val)
            for k, (sem, v) in need.items():
                if waited.get(k, 0) < v:
                    eng.wait_ge(sem, v)
                    waited[k] = v
            r = ins.fn(eng)
            if ins.isdma:
                r.then_inc(ins.sem, 16)
            elif ins.sig:
                r.then_inc(ins.sem, 1)
        if e == "sp":
            for ins in self.final:
                k = id(ins.sem)
                if waited.get(k, 0) < ins.val:
                    eng.wait_ge(ins.sem, ins.val)
                    waited[k] = ins.val


def build_nc(nb=4, depth=DEPTH, dbg=False):
    nc = bass.Bass("TRN2", target_bir_lowering=False)
    es = ExitStack()
    ntok = nb * T

    def din(name, shape, dt):
        return nc.dram_tensor(name, shape, dt, kind="ExternalInput").ap()

    x_d = din("x", [S, D], F32)
    pos_d = din("pos", [128, S], I32)
    cst_d = din("cst", [128, NCST], F32)
    cmat_d = din("cmat", [128, 5, 128], F32)
    postw_d = din("postw", [DEPTH, 128, D], F32)
    win_d = din("w_in", [DEPTH, D, DIN], F32)
    wout_d = din("w_out", [DEPTH, D, D], F32)
    wuq_d = din("w_uq", [DEPTH, 512, 768], F32)
    wukv_d = din("w_ukv", [DEPTH, 512, 1024], F32)
    out_d = nc.dram_tensor("out", [S, D], F32, kind="ExternalOutput").ap()
    dbg_d = nc.dram_tensor("dbg", [8, 128, 2048], F32, kind="ExternalOutput").ap() if dbg else None

    def dint(name, shape, dt):
        return nc.dram_tensor(name, shape, dt, kind="Internal").ap()

    cs_d = dint("cs_d", [2, 128, S], F32)
    x1_d = dint("x1_d", [S, D], F32)
    kA_d = [dint("kA_d%d" % l, [4, 128, S], BF16) for l in range(depth)]
    vA_d = [dint("vA_d%d" % l, [S, 512], BF16) for l in range(depth)]
    knC_d = [dint("knC_d%d" % l, [4, 128, S], BF16) for l in range(depth)]
    krC_d = [dint("krC_d%d" % l, [128, S], BF16) for l in range(depth)]
    vC_d = [dint("vC_d%d" % l, [S, 512], BF16) for l in range(depth)]

    sc = Sched(nc, es)

    def sb(name, shape, dt):
        return es.enter_context(nc.sbuf_tensor("sb_" + name, shape, dt))

    def ps(name, shape, dt):
        return es.enter_context(nc.psum_tensor("ps_" + name, shape, dt))

    cst = sb("cst", [128, NCST], F32); t_cst = Tl("cst")
    cmb = sb("cmb", [128, 5, 128], BF16); t_cmb = Tl("cmb")
    ident, RT, tri, atri, ones = (cmb[:, i, :] for i in range(5))
    negb = sb("negb", [128, 256], BF16); t_negb = Tl("negb")
    postw = sb("postw", [128, D], F32); t_postw = Tl("postw")
    cs = sb("cs", [128, 2, T], F32); t_cs = Tl("cs")
    XR = [sb("xr%d" % i, [128, D], F32) for i in range(2)]; t_XR = [Tl("xr%d" % i) for i in range(2)]
    hb = sb("hb", [128, D], BF16); t_hb = Tl("hb")
    st = sb("st", [128, 32], F32); t_st = Tl("st")
    lam = sb("lam", [128, 8], F32); t_lam = Tl("lam")
    lamt = sb("lamt", [128, 4, 64], F32)
    esink = sb("esink", [128, 16], F32); t_esink = Tl("esink")

    BIG = sb("big", [128, 4608], F32)
    bigb = BIG[:].bitcast(BF16)
    hT = bigb[:, 0:8192].rearrange("p (k t) -> p k t", k=16); t_hT = Tl("hT")
    qraw = [bigb[:, 8192 + i * 512:8192 + (i + 1) * 512] for i in range(2)]; t_qraw = [Tl("qraw%d" % i) for i in range(2)]
    BIG2 = sb("big2", [128, 8192], F32)
    OB = BIG2[:].rearrange("p (s d) -> p s d", s=4)
    b2b = BIG2[:].bitcast(BF16)
    cf = BIG2[:, 0:2048].rearrange("p (k t) -> p k t", k=4); t_cf = Tl("cf")
    cn = b2b[:, 4096:6144].rearrange("p (k t) -> p k t", k=4); t_cn = Tl("cn")
    sq = [b2b[:, 6144 + i * 512:6144 + (i + 1) * 512] for i in range(2)]; t_sq = [Tl("sq%d" % i) for i in range(2)]
    npre = max(128, (nb - 1) * T)
    assert npre <= 1536
    NE = 4
    E = [b2b[:, 7168:7680], b2b[:, 7680:8192], b2b[:, 11264:11776], sb("E3", [128, T], BF16)[:]]
    t_E = [Tl("E%d" % i) for i in range(NE)]
    kpre = [b2b[:, 8192 + i * 1536:8192 + i * 1536 + npre] for i in range(2)]; t_kpre = [Tl("kpre%d" % i) for i in range(2)]
    vpre = [b2b[:, 12288 + i * 1536:12288 + i * 1536 + npre].rearrange("p (kb d) -> p kb d", d=128) for i in range(2)]
    t_vpre = [Tl("vpre%d" % i) for i in range(2)]
    krpre = sb("krpre", [128, npre], BF16)[:]; t_krpre = Tl("krpre")
    t_OB = [[t_cf],
            [t_cn, t_sq[0], t_sq[1], t_E[0], t_E[1]],
            [t_kpre[0], t_kpre[1], t_E[2]],
            [t_vpre[0], t_vpre[1]]]
    t_big2 = [t_cf, t_cn] + t_sq + t_kpre + t_vpre + t_E[0:3]

    Q = sb("Q", [128, 8, T], BF16); t_Q = [Tl("Q%d" % i) for i in range(8)]
    yT = sb("yT", [128, 16, T], BF16); t_yT = [Tl("yT%d" % i) for i in range(16)]
    kAc = sb("kAc", [128, 4, T], BF16); t_kAc = [Tl("kAc%d" % i) for i in range(4)]
    vAc = sb("vAc", [128, 4, 512], BF16); t_vAc = Tl("vAc")
    knCc = sb("knCc", [128, 4, T], BF16); t_knCc = [Tl("knCc%d" % i) for i in range(4)]
    krCc = sb("krCc", [128, T], BF16); t_krCc = Tl("krCc")
    vCc = sb("vCc", [128, 4, 512], BF16); t_vCc = Tl("vCc")
    kBz = [[sb("kBz%d%d" % (g_, r_), [128, 640], BF16) for r_ in range(2)] for g_ in range(2)]; t_kB = Tl("kB")
    vB = sb("vB", [128, 16, 128], BF16); t_vB = Tl("vB")
    NW = 4
    WALL = sb("wall", [128, NW, KC, 256], BF16)
    W = [WALL[:, i] for i in range(NW)]; t_W = [Tl("W%d" % i) for i in range(NW)]
    wuqn = sb("wuqn", [128, 4, 4, 128], BF16); t_wuqn = Tl("wuqn")
    wuqr = sb("wuqr", [128, 4, 4, 64], BF16); t_wuqr = Tl("wuqr")
    wukk = sb("wukk", [128, 4, 4, 128], BF16); t_wukk = Tl("wukk")
    wukv = sb("wukv", [128, 4, 4, 128], BF16); t_wukv = Tl("wukv")
    t1 = [sb("t1_%d" % i, [128, T], F32) for i in range(2)]; t_t1 = [Tl("t1_%d" % i) for i in range(2)]
    t2 = [sb("t2_%d" % i, [128, T], F32) for i in range(2)]; t_t2 = [Tl("t2_%d" % i) for i in range(2)]
    rd = [sb("rd%d" % i, [128, T], F32) for i in range(2)]; t_rd = [Tl("rd%d" % i) for i in range(2)]
    o1 = sb("o1", [128, T], F32); t_o1 = Tl("o1")
    df = sb("df", [128, T], F32); t_df = Tl("df")
    nrm = sb("nrm", [128, T], F32); t_nrm = Tl("nrm")

    G = [ps("g%d" % i, [128, 512], F32) for i in range(3)]; t_G = [Tl("g%d" % i, True) for i in range(3)]
    NS_ = 3
    SP_ = [ps("s%d" % i, [128, 512], F32) for i in range(NS_)]; t_S = [Tl("s%d" % i, True) for i in range(NS_)]
    ao = ps("ao", [128, 512], F32); t_ao = Tl("ao", True)
    ad = ps("ad", [128, 512], F32); t_ad = Tl("ad", True)

    t_csd = Tl("cs_d")
    t_x1 = [Tl("x1_%d" % c) for c in range(nb)]
    t_kv = [[{n: Tl("%s_%d_%d" % (n, l, c)) for n in ("kA", "vA", "knC", "krC", "vC")} for c in range(nb)] for l in range(depth)]

    cnt = {"g": 0, "e": 0, "s": 0, "rp": 0, "rd": 0, "w": 0}
    PDEPTH = 2
    pending_rope = []

    def flush_rope():
        while pending_rope:
            pending_rope.pop(0)()

    def nxt(k, n):
        v = cnt[k] % n
        cnt[k] += 1
        return v

    def mm(out, lhsT, rhs, start, stop, reads, writes, skip=False):
        if skip:
            sc.op("pe", lambda e: e.matmul(out, lhsT=lhsT, rhs=rhs, start=start, stop=stop, skip_group_check=True),
                  reads=reads, writes=writes)
        else:
            sc.op("pe", lambda e: e.matmul(out, lhsT=lhsT, rhs=rhs, start=start, stop=stop), reads=reads, writes=writes)

    def act(out, in_, func, reads, writes, **kw):
        sc.op("act", lambda e: e.activation(out=out, in_=in_, func=func, **kw), reads=reads, writes=writes)

    def dve_tt(out, in0, in1, op, reads, writes):
        sc.op("dve", lambda e: e.tensor_tensor(out=out, in0=in0, in1=in1, op=op), reads=reads, writes=writes)

    def dve_ts(out, in0, s1, s2, op0, op1, reads, writes):
        if op1 is None:
            sc.op("dve", lambda e: e.tensor_scalar(out=out, in0=in0, scalar1=s1, scalar2=None, op0=op0), reads=reads, writes=writes)
        else:
            sc.op("dve", lambda e: e.tensor_scalar(out=out, in0=in0, scalar1=s1, scalar2=s2, op0=op0, op1=op1), reads=reads, writes=writes)

    def dve_stt(out, in0, scalar, in1, op0, op1, reads, writes):
        sc.op("dve", lambda e: e.scalar_tensor_tensor(out=out, in0=in0, scalar=scalar, in1=in1, op0=op0, op1=op1), reads=reads, writes=writes)

    def pool_tt(out, in0, in1, op, reads, writes):
        sc.op("pool", lambda e: e.tensor_tensor(out=out, in0=in0, in1=in1, op=op), reads=reads, writes=writes)

    late_ops = []

    def flush_late():
        while late_ops:
            late_ops.pop(0)()

    def dve_rcp(out, in_, reads, writes):
        sc.op("dve", lambda e: e.reciprocal(out=out, in_=in_), reads=reads, writes=writes)

    def dve_cp(out, in_, reads, writes):
        sc.op("dve", lambda e: e.tensor_copy(out=out, in_=in_), reads=reads, writes=writes)

    def dma_sp(out, in_, sbt, reads, writes, final=False):
        sc.dma("sp", lambda e: e.dma_start(out=out, in_=in_), sbt, reads=reads, writes=writes, final=final)

    def dma_pool(out, in_, sbt, reads, writes):
        sc.dma("pool", lambda e: e.dma_start(out=out, in_=in_), sbt, reads=reads, writes=writes)

    dbg_n = [0]

    def dump(ap, tls, rows=128, cols=None):
        if not dbg or dbg_n[0] >= 8:
            return
        i = dbg_n[0]
        dbg_n[0] += 1
        cols = cols or ap.shape[-1]
        tmpt = sb("dbgt%d" % i, [128, cols], F32)
        tt = Tl("dbgt%d" % i)
        dve_cp(tmpt[0:rows, :], ap, tls, [tt])
        dma_sp(dbg_d[i, 0:rows, 0:cols], tmpt[0:rows, :], tt, [tt], [], final=True)
        return i

    wq = []

    def wq_add(parts):
        wq.append(parts)
        return len(wq) - 1

    win_v = [win_d[l].rearrange("(k p) c -> p k c", p=128) for l in range(DEPTH)]
    wout_v = [wout_d[l].rearrange("(k p) c -> p k c", p=128) for l in range(DEPTH)]

    def block_chunks(l):
        ch = []
        for c0 in range(0, 2048, 256):
            ch.append([(0, 256, win_v[l][:, :, c0:c0 + 256])])
        for c0 in range(2048, 3072, 256):
            ch.append([(0, 256, win_v[l][:, :, c0:c0 + 256])])
        ch.append([(0, 64, win_v[l][:, :, 3072:3136]), (64, 64, win_v[l][:, :, 3072:3136]),
                   (128, 64, win_v[l][:, :, 3136:3200]), (192, 64, win_v[l][:, :, 3136:3200])])
        ch.append([(0, 128, win_v[l][:, :, 3200:3328])])
        for c0 in range(3328, 4352, 256):
            ch.append([(0, 256, win_v[l][:, :, c0:c0 + 256])])
        for c0 in range(4352, 4864, 256):
            ch.append([(0, 256, win_v[l][:, :, c0:c0 + 256])])
        ch.append([(0, 64, win_v[l][:, :, 5376:5440]), (64, 64, win_v[l][:, :, 5376:5440])])
        for c0 in range(5440, 5952, 256):
            ch.append([(0, 256, win_v[l][:, :, c0:c0 + 256])])
        for c0 in range(4864, 5376, 256):
            ch.append([(0, 256, win_v[l][:, :, c0:c0 + 256])])
        for c0 in range(0, 2048, 256):
            ch.append([(0, 256, wout_v[l][:, :, c0:c0 + 256]), "pair%d" % ((c0 // 256) % 2)])
        return ch

    wslot = []
    _sc = 0
    for l in range(depth):
        for c in range(nb):
            for parts in block_chunks(l):
                tag = None
                if isinstance(parts[-1], str):
                    tag = parts[-1]
                    parts = parts[:-1]
                if tag == "pair0" and _sc % 2 == 1:
                    _sc += 1
                wq_add(parts)
                wslot.append(_sc % NW)
                _sc += 1
    wstate = {"issued": 0, "cur": -1}

    def w_issue_upto(i):
        while wstate["issued"] <= min(i, len(wq) - 1):
            j = wstate["issued"]
            slot = wslot[j]
            if any(wslot[q] == slot for q in range(max(wstate.get("protect", 0), 0), j)):
                break
            for (d0, n, src) in wq[j]:
                dma_pool(W[slot][:, :, d0:d0 + n], src, t_W[slot], [], [t_W[slot]])
            wstate["issued"] += 1

    def w_next():
        wstate["cur"] += 1
        i = wstate["cur"]
        wstate["protect"] = i
        w_issue_upto(i + NW - 1)
        return W[wslot[i]], t_W[wslot[i]]

    def w_next_pair():
        wstate["cur"] += 2
        i0 = wstate["cur"] - 1
        wstate["protect"] = i0
        w_issue_upto(i0 + NW - 1)
        return i0

    dma_sp(cst[:], cst_d, t_cst, [], [t_cst])
    dma_pool(cmb[:], cmat_d, t_cmb, [], [t_cmb])
    w_issue_upto(NW - 2)
    for g_ in range(2):
        for r_ in range(2):
            sc.op("dve", lambda e, g_=g_, r_=r_: e.memset(kBz[g_][r_][:], 0.0), reads=[], writes=[t_kB])
    dve_ts(negb[:, 0:128], tri, -1.0, 30000.0, ALU.add, ALU.mult, [t_cmb], [t_negb])
    dve_ts(negb[:, 128:256], atri, -1.0, 30000.0, ALU.add, ALU.mult, [t_cmb], [t_negb])
    pi_t = XR[0][:].bitcast(I32)
    dma_sp(pi_t, pos_d, t_XR[0], [], [t_XR[0]])
    pf = XR[1]
    kf = OB[:, 0, :]
    ki = OB[:, 1, :].bitcast(I32)
    sa = OB[:, 2, :]
    ca = OB[:, 3, :]
    tb = t_big2
    dve_cp(pf[:], pi_t, [t_XR[0]], [t_XR[1]])
    dve_ts(pf[:], pf[:], cst[:, C_INVF:C_INVF + 1], None, ALU.mult, None, [t_XR[1], t_cst], [t_XR[1]])
    dve_ts(kf, pf[:], 1.0 / (2 * PI), None, ALU.mult, None, [t_XR[1]], tb)
    dve_cp(ki, kf, tb, tb)
    dve_cp(kf, ki, tb, tb)
    c1 = 6.28125
    c2 = float(np.float32(2 * PI - c1))
    dve_stt(pf[:], kf, -c1, pf[:], ALU.mult, ALU.add, tb + [t_XR[1]], [t_XR[1]])
    dve_stt(pf[:], kf, -c2, pf[:], ALU.mult, ALU.add, tb + [t_XR[1]], [t_XR[1]])
    dve_ts(kf, pf[:], PI, -2 * PI, ALU.is_gt, ALU.mult, [t_XR[1]], tb)
    dve_tt(sa, pf[:], kf, ALU.add, [t_XR[1]] + tb, tb)
    dve_ts(pf[:], pf[:], PI / 2, None, ALU.add, None, [t_XR[1]], [t_XR[1]])
    dve_ts(kf, pf[:], PI, -2 * PI, ALU.is_gt, ALU.mult, [t_XR[1]], tb)
    dve_tt(ca, pf[:], kf, ALU.add, [t_XR[1]] + tb, tb)
    act(OB[:, 2:4, :], OB[:, 2:4, :], AF.Sin, tb, tb)
    dma_sp(cs_d[0], ca, t_cf, tb, [t_csd])
    dma_sp(cs_d[1], sa, t_cf, tb, [t_csd])

    def layer_setup(l):
        dma_sp(postw[:], postw_d[l], t_postw, [], [t_postw])
        wq_v = wuq_d[l].rearrange("(k p) (h d) -> p k h d", p=128, d=192)
        wk_v = wukv_d[l].rearrange("(k p) (h d) -> p k h d", p=128, d=256)
        for k in range(4):
            dma_pool(wuqn[:, k], wq_v[:, k, :, 0:128], t_wuqn, [], [t_wuqn])
            dma_pool(wuqr[:, k], wq_v[:, k, :, 128:192], t_wuqr, [], [t_wuqr])
            dma_pool(wukk[:, k], wk_v[:, k, :, 0:128], t_wukk, [], [t_wukk])
            dma_pool(wukv[:, k], wk_v[:, k, :, 128:256], t_wukv, [], [t_wukv])
        lam_init = 0.8 - 0.6 * math.exp(-0.3 * l)
        base = C_LAM + l * 256
        lv = cst[:, base:base + 256].rearrange("p (v i) -> p v i", v=4)
        dve_tt(lamt[:, 0, :], lv[:, 0, :], lv[:, 1, :], ALU.mult, [t_cst], [t_lam])
        dve_tt(lamt[:, 1, :], lv[:, 2, :], lv[:, 3, :], ALU.mult, [t_cst], [t_lam])
        sc.op("dve", lambda e: e.reduce_sum(out=lam[:, 2:4], in_=lamt[:, 0:2, :], axis=mybir.AxisListType.X), reads=[t_lam], writes=[t_lam])
        act(lam[:, 4:6], lam[:, 2:4], AF.Exp, [t_lam], [t_lam])
        dve_stt(lam[:, 0:1], lam[:, 5:6], -lam_init, lam[:, 4:5], ALU.add, ALU.subtract, [t_lam], [t_lam])
        dve_ts(lam[:, 1:2], cst[:, C_SUBLN + l:C_SUBLN + l + 1], 1.0 - lam_init, None, ALU.mult, None, [t_cst, t_lam], [t_lam])
        act(esink[:, 0:8], cst[:, C_SINK + l * 8:C_SINK + l * 8 + 8], AF.Exp, [t_cst], [t_esink])

    def p0_cs(l, c):
        tok0 = c * T
        dma_sp(cs[:], cs_d[:, :, tok0:tok0 + T].rearrange("a p t -> p a t"), t_cs, [t_csd], [t_cs])

    def p0_stats(l, c, s):
        tok0 = c * T
        src = x_d if l == 0 else x1_d
        xr, txr = XR[s % 2], t_XR[s % 2]
        r0 = tok0 + s * 128
        dma_sp(xr[:], src[r0:r0 + 128, :], txr, [t_x1[c]] if l > 0 else [], [txr])
        sc.op("act", lambda e, xr=xr, s=s: e.activation(out=hb[:], in_=xr[:], func=AF.Square, accum_out=st[:, s:s + 1]),
              reads=[txr], writes=[t_hb, t_st])
        act(st[:, 4 + s:5 + s], st[:, s:s + 1], AF.Sqrt, [t_st], [t_st], scale=1.0 / D, bias=1e-6)
        dve_rcp(st[:, 8 + s:9 + s], st[:, 4 + s:5 + s], [t_st], [t_st])
        dve_ts(hb[:], xr[:], st[:, 8 + s:9 + s], None, ALU.mult, None, [txr, t_st], [t_hb])

    def p0_tr(l, c, s):
        for g in range(4):
            gi = nxt("g", 2)
            tp = G[gi][:].bitcast(BF16).rearrange("p (j t) -> p j t", j=8)[:, 0:4, :]
            for j in range(4):
                k = g * 4 + j
                sc.op("pe", lambda e, tp=tp, j=j, k=k: e.transpose(out=tp[:, j, :], in_=hb[:, k * 128:(k + 1) * 128], identity=ident),
                      reads=[t_hb, t_cmb], writes=[t_G[gi]])
            pw = cst[:, C_PREW + l * 16 + g * 4:C_PREW + l * 16 + g * 4 + 4].unsqueeze(2).broadcast_to([128, 4, 128])
            dve_tt(hT[:, g * 4:(g + 1) * 4, s * 128:(s + 1) * 128], tp, pw, ALU.mult, [t_G[gi], t_cst], [t_hT])

    def phase0(l, c):
        p0_cs(l, c)
        for s in range(NSB):
            p0_stats(l, c, s)
            p0_tr(l, c, s)

    def phase0_bg(l, c):
        def mk(i):
            def f():
                if i == 0:
                    p0_cs(l, c)
                if i > 0:
                    p0_tr(l, c, i - 1)
                if i < NSB:
                    p0_stats(l, c, i)
            return f
        return [mk(i) for i in range(NSB + 1)]

    def fm_group(wsl, twsl, col0, m=128):
        gi = nxt("g", 2)
        for k in range(KC):
            mm(G[gi][0:m, :], wsl[:, k, col0:col0 + m], hT[:, k, :], k == 0, k == KC - 1, [twsl, t_hT], [t_G[gi]])
        flush_rope()
        flush_late()
        return gi

    def rope_evac(gi, dst, tdst):
        r = nxt("rp", 2)
        act(qraw[r], G[gi][:], AF.Copy, [t_G[gi]], [t_qraw[r]])
        dve_tt(t1[r][:], G[gi][:], cs[:, 0, :], ALU.mult, [t_G[gi], t_cs], [t_t1[r]])

        def stage_b(r=r, dst=dst, tdst=tdst):
            mm(G[2][:], RT, qraw[r], True, True, [t_cmb, t_qraw[r]], [t_G[2]])
            dve_tt(t2[r][:], G[2][:], cs[:, 1, :], ALU.mult, [t_G[2], t_cs], [t_t2[r]])
            if isinstance(dst, list):
                for (rw, ap) in dst:
                    dve_tt(ap, t1[r][rw, :], t2[r][rw, :], ALU.add, [t_t1[r], t_t2[r]], tdst)
            else:
                dve_tt(dst, t1[r][:], t2[r][:], ALU.add, [t_t1[r], t_t2[r]], tdst)
        pending_rope.append(stage_b)

    def tm_group(wsl, twsl, ncols, dst_fn, tdst, src=None, tsrc=None, nk=KC, wfn=None):
        flush_rope()
        for s in range(NSB):
            gi = nxt("g", 2)
            for k in range(nk):
                lhsT = (hT if src is None else src)[:, k, s * 128:(s + 1) * 128]
                rhs = wsl[:, k, 0:ncols] if wfn is None else wfn(k)
                mm(G[gi][:, 0:ncols], lhsT, rhs, k == 0, k == nk - 1, [twsl, t_hT if src is None else tsrc], [t_G[gi]])
            act(dst_fn(s), G[gi][:, 0:ncols], AF.Copy, [t_G[gi]], tdst)

    def attn_head(kparts, vfn, scale, c, nprefix_kb, finalize, bg=None):
        flush_rope()
        nkb = nprefix_kb + 4
        pend = []

        def stage2(kb, ei, q0):
            vl, vt = vfn(kb)
            mm(ao[:, q0:T], vl, E[ei][:, q0:T], kb == 0, kb == nkb - 1, vt + [t_E[ei]], [t_ao])
            mm(ad[:, q0:T], ones, E[ei][:, q0:T], kb == 0, kb == nkb - 1, [t_cmb, t_E[ei]], [t_ad])

        for kb in range(nkb):
            q0 = max(0, (kb - nprefix_kb) * 128)
            si = nxt("s", NS_)
            parts = kparts(kb)
            diag = kb >= nprefix_kb
            if diag:
                mm(SP_[si][:, q0:q0 + 128], ident, negb[:, 0:128], True, False, [t_cmb, t_negb], [t_S[si]])
            for pi_, (lh, rhf, tls) in enumerate(parts):
                mm(SP_[si][:, q0:T], lh, rhf(q0), (pi_ == 0) and not diag, pi_ == len(parts) - 1, tls, [t_S[si]], skip=diag)
            ei = nxt("e", NE)
            act(E[ei][:, q0:T], SP_[si][:, q0:T], AF.Exp, [t_S[si]], [t_E[ei]], scale=scale)
            pend.append((kb, ei, q0))
            if len(pend) > PDEPTH:
                stage2(*pend.pop(0))
            if kb == min(3, nkb - 1):
                flush_late()
            if bg is not None and kb == 1:
                bg()
        for p in pend:
            stage2(*p)
        finalize()

    def acc_out(use_dve=True):
        r = nxt("rd", 2)
        act(rd[r][:], ad[:], AF.Copy, [t_ad], [t_rd[r]])
        if use_dve:
            dve_cp(t1[r][:], ao[:], [t_ao], [t_t1[r]])
        else:
            act(t1[r][:], ao[:], AF.Copy, [t_ao], [t_t1[r]])
        return r

    def phaseA(l, c):
        tok0 = c * T
        last = (c == nb - 1)
        sc.op("pool", lambda e: e.memset(Q[64:128, 0:4, :], 0.0), reads=[], writes=t_Q[0:4])
        sc.op("pool", lambda e: e.memset(Q[0:64, 4:8, :], 0.0), reads=[], writes=t_Q[4:8])
        for i in range(2):
            wsl, tw = w_next()
            for h in range(2):
                gi = fm_group(wsl, tw, h * 128)
                jj = i * 2 + h
                rope_evac(gi, [(slice(0, 64), Q[0:64, jj, :]), (slice(64, 128), Q[64:128, 4 + jj, :])], [t_Q[jj], t_Q[4 + jj]])
        for i in range(2):
            wsl, tw = w_next()
            for h in range(2):
                gi = fm_group(wsl, tw, h * 128)
                rope_evac(gi, kAc[:, i * 2 + h, :], [t_kAc[i * 2 + h]])
        for i in range(2):
            wsl, tw = w_next()
            tm_group(wsl, tw, 256, lambda s, i=i: vAc[:, s, i * 256:(i + 1) * 256], [t_vAc])
        for i in range(2):
            wsl, tw = w_next()
            for h in range(2):
                gi = fm_group(wsl, tw, h * 128)
                act(yT[:, i * 2 + h, :], G[gi][:], AF.Silu, [t_G[gi]], [t_yT[i * 2 + h]])
        if not last:
            dma_sp(kA_d[l].rearrange("j p t -> p j t")[:, :, tok0:tok0 + T], kAc[:], t_kAc[0], t_kAc, [t_kv[l][c]["kA"]])
            dma_sp(vA_d[l][tok0:tok0 + T, :].rearrange("(s p) d -> p s d", p=128), vAc[:], t_vAc, [t_vAc], [t_kv[l][c]["vA"]])
        npk = c * 4
        sc.marks.append(("Aa", l, c, len(sc.streams["pe"])))
        for j in range(4):
            sl = j % 2
            if c > 0:
                dma_sp(kpre[sl][:, 0:c * T], kA_d[l][j, :, 0:c * T], t_kpre[sl], [t_kv[l][cc]["kA"] for cc in range(c)], [t_kpre[sl]])
                dma_sp(vpre[sl][:, 0:npk, :], vA_d[l][0:c * T, j * 128:(j + 1) * 128].rearrange("(kb p) d -> p kb d", p=128),
                       t_vpre[sl], [t_kv[l][cc]["vA"] for cc in range(c)], [t_vpre[sl]])
            for comp in range(2):
                rows = slice(comp * 64, comp * 64 + 64)

                def kparts(kb, comp=comp, j=j, sl=sl):
                    if kb < npk:
                        lh = kpre[sl][:, kb * 128:(kb + 1) * 128]
                        tl = [t_kpre[sl]]
                    else:
                        kk = kb - npk
                        lh = kAc[:, j, kk * 128:(kk + 1) * 128]
                        tl = [t_kAc[j]]
                    qi = comp * 4 + j
                    return [(lh, lambda q0: Q[:, qi, q0:T], tl + [t_Q[qi]])]

                def vfn(kb, j=j, sl=sl):
                    if kb < npk:
                        return vpre[sl][:, kb, :], [t_vpre[sl]]
                    return vAc[:, kb - npk, j * 128:(j + 1) * 128], [t_vAc]

                def fin(comp=comp, j=j):
                    r = acc_out()
                    dve_rcp(rd[r][:], rd[r][:], [t_rd[r]], [t_rd[r]])
                    if comp == 0:
                        dve_tt(o1[:], t1[r][:], rd[r][:], ALU.mult, [t_t1[r], t_rd[r]], [t_o1])
                    else:
                        dve_tt(df[:], t1[r][:], rd[r][:], ALU.mult, [t_t1[r], t_rd[r]], [t_df])
                        dve_stt(df[:], df[:], lam[:, 0:1], o1[:], ALU.mult, ALU.add, [t_df, t_lam, t_o1], [t_df])
                        def late(j=j):
                            act(sq[0], df[:], AF.Square, [t_df], [t_sq[0]])
                            mm(G[2][:], ones, sq[0], True, True, [t_cmb, t_sq[0]], [t_G[2]])
                            act(nrm[:], G[2][:], AF.Sqrt, [t_G[2]], [t_nrm], scale=1.0 / 128, bias=1e-5)
                            dve_rcp(nrm[:], nrm[:], [t_nrm], [t_nrm])
                            dve_stt(df[:], df[:], lam[:, 1:2], nrm[:], ALU.mult, ALU.mult, [t_df, t_lam, t_nrm], [t_df])
                            dve_tt(yT[:, j, :], df[:], yT[:, j, :], ALU.mult, [t_df, t_yT[j]], [t_yT[j]])
                        late_ops.append(late)

                attn_head(kparts, vfn, 0.125, c, npk, fin)

    def phaseB(l, c):
        tok0 = c * T
        for i in range(4):
            wsl, tw = w_next()
            for h in range(2):
                gi = fm_group(wsl, tw, h * 128)
                rope_evac(gi, Q[:, i * 2 + h, :], [t_Q[i * 2 + h]])
        wsl, tw = w_next()
        for g in range(2):
            gi = fm_group(wsl, tw, g * 128)
            rope_evac(gi, [(slice(0, 64), kBz[g][0][0:64, 128:640]), (slice(64, 128), kBz[g][1][64:128, 128:640])], [t_kB])
        wsl, tw = w_next()
        tm_group(wsl, tw, 128, lambda s: vB[:, c * 4 + s, :], [t_vB])
        for i in range(4):
            wsl, tw = w_next()
            for h in range(2):
                gi = fm_group(wsl, tw, h * 128)
                act(yT[:, 4 + i * 2 + h, :], G[gi][:], AF.Silu, [t_G[gi]], [t_yT[4 + i * 2 + h]])
        flush_rope()
        sc.marks.append(("Ba", l, c, len(sc.streams["pe"])))
        for i in range(8):
            g = i // 4
            pend = []

            def stage2(kbg, ei, qb_lo, qb_hi, g=g):
                for r in range(2):
                    rows = slice(r * 64, r * 64 + 64)
                    for qb in range(qb_lo, qb_hi + 1):
                        qq = (qb - c * 4) * 128
                        ec = r * 256 + (qb - qb_lo) * 128
                        first = (kbg == qb - 1) or (qb == 0)
                        lastk = (kbg == qb)
                        mm(ao[rows, qq:qq + 128], vB[:, kbg, g * 64:(g + 1) * 64], E[ei][:, ec:ec + 128], first, lastk,
                           [t_vB, t_E[ei]], [t_ao])
                        mm(ad[rows, qq:qq + 128], ones[:, 0:64], E[ei][:, ec:ec + 128], first, lastk,
                           [t_cmb, t_E[ei]], [t_ad])

            for kbg in range(c * 4 - 1, c * 4 + 4):
                if kbg < 0:
                    continue
                qb_lo = max(kbg, c * 4)
                qb_hi = min(kbg + 1, c * 4 + 3)
                q0 = (qb_lo - c * 4) * 128
                q1 = (qb_hi - c * 4 + 1) * 128
                N = q1 - q0
                si = nxt("s", NS_)
                kc0 = (kbg - c * 4) * 128 + 128
                if N == 256:
                    nsel = negb[:, 0:256]
                elif qb_lo == kbg:
                    nsel = negb[:, 0:128]
                else:
                    nsel = negb[:, 128:256]
                for r in range(2):
                    mm(SP_[si][:, r * 256:r * 256 + N], ident, nsel, True, False, [t_cmb, t_negb], [t_S[si]])
                    mm(SP_[si][:, r * 256:r * 256 + N], kBz[g][r][:, kc0:kc0 + 128], Q[:, i, q0:q1], False, True,
                       [t_kB, t_Q[i]], [t_S[si]])
                ei = nxt("e", NE)
                sv = SP_[si][:].rearrange("p (r n) -> p r n", r=2)[:, :, 0:N]
                ev = E[ei][:].rearrange("p (r n) -> p r n", r=2)[:, :, 0:N]
                act(ev, sv, AF.Exp, [t_S[si]], [t_E[ei]], scale=0.125)
                pend.append((kbg, ei, qb_lo, qb_hi))
                if len(pend) > PDEPTH:
                    stage2(*pend.pop(0))
            for p in pend:
                stage2(*p)
            rr = acc_out(use_dve=False)
            dve_ts(rd[rr][:], rd[rr][:], esink[:, i:i + 1], None, ALU.add, None, [t_rd[rr], t_esink], [t_rd[rr]])
            dve_rcp(rd[rr][:], rd[rr][:], [t_rd[rr]], [t_rd[rr]])
            dve_tt(df[:], t1[rr][:], rd[rr][:], ALU.mult, [t_t1[rr], t_rd[rr]], [t_df])
            dve_tt(yT[:, 4 + i, :], df[:], yT[:, 4 + i, :], ALU.mult, [t_df, t_yT[4 + i]], [t_yT[4 + i]])
        for g in range(2):
            for r in range(2):
                rw = slice(r * 64, r * 64 + 64)
                dve_cp(kBz[g][r][rw, 0:128], kBz[g][r][rw, 512:640], [t_kB], [t_kB])

    sq4 = [sq[0], sq[1], E[0], E[1]]
    t_sq4 = [t_sq[0], t_sq[1], t_E[0], t_E[1]]

    def c_norm_s1():
        for k in range(4):
            act(sq4[k], cf[:, k, :], AF.Square, [t_cf], [t_sq4[k]])

    def c_norm_s2(l, col):
        flush_rope()
        for k in range(4):
            mm(G[2][:], ones, sq4[k], k == 0, k == 3, [t_cmb, t_sq4[k]], [t_G[2]])
        act(nrm[:], G[2][:], AF.Sqrt, [t_G[2]], [t_nrm], scale=1.0 / 512, bias=1e-6)
        dve_rcp(nrm[:], nrm[:], [t_nrm], [t_nrm])
        for k in range(4):
            dve_stt(cn[:, k, :], cf[:, k, :], cst[:, col + k:col + k + 1], nrm[:], ALU.mult, ALU.mult, [t_cf, t_cst, t_nrm], [t_cn])

    def phaseC_in(l, c):
        tok0 = c * T
        last = (c == nb - 1)
        for i_ in range(2):
            sc.op("pool", lambda e, i_=i_: e.memset(Q[64:128, 4 + 2 * i_, :], 0.0), reads=[], writes=[t_Q[4 + 2 * i_]])
            sc.op("pool", lambda e, i_=i_: e.memset(Q[0:64, 5 + 2 * i_, :], 0.0), reads=[], writes=[t_Q[5 + 2 * i_]])
        for i in range(2):
            wsl, tw = w_next()
            for h in range(2):
                gi = fm_group(wsl, tw, h * 128)
                act(cf[:, i * 2 + h, :], G[gi][:], AF.Copy, [t_G[gi]], [t_cf])
        c_norm_s1()
        wsl, tw = w_next()
        gi = fm_group(wsl, tw, 0)
        rope_evac(gi, krCc[:], [t_krCc])
        c_norm_s2(l, C_QNW + l * 4)
        for i in range(2):
            wsl, tw = w_next()
            for h in range(2):
                gi = fm_group(wsl, tw, h * 128)
                act(yT[:, 12 + i * 2 + h, :], G[gi][:], AF.Silu, [t_G[gi]], [t_yT[12 + i * 2 + h]])
        flush_rope()
        for i in range(2):
            wsl, tw = w_next()
            for h in range(2):
                gi = fm_group(wsl, tw, h * 128)
                act(cf[:, i * 2 + h, :], G[gi][:], AF.Copy, [t_G[gi]], [t_cf])
        c_norm_s1()
        for j in range(4):
            gi = nxt("g", 2)
            for k in range(4):
                mm(G[gi][:], wuqn[:, k, j, :], cn[:, k, :], k == 0, k == 3, [t_wuqn, t_cn], [t_G[gi]])
            act(Q[:, j, :], G[gi][:], AF.Copy, [t_G[gi]], [t_Q[j]])
        for i in range(2):
            gi = nxt("g", 2)
            for k in range(4):
                mm(G[gi][:], wuqr[:, k, 2 * i:2 * i + 2, :], cn[:, k, :], k == 0, k == 3, [t_wuqr, t_cn], [t_G[gi]])
            flush_rope()
            rope_evac(gi, [(slice(0, 64), Q[0:64, 4 + 2 * i, :]), (slice(64, 128), Q[64:128, 5 + 2 * i, :])],
                      [t_Q[4 + 2 * i], t_Q[5 + 2 * i]])
        c_norm_s2(l, C_KVNW + l * 4)
        for j in range(4):
            gi = nxt("g", 2)
            for k in range(4):
                mm(G[gi][:], wukk[:, k, j, :], cn[:, k, :], k == 0, k == 3, [t_wukk, t_cn], [t_G[gi]])
            act(knCc[:, j, :], G[gi][:], AF.Copy, [t_G[gi]], [t_knCc[j]])
        tm_group(None, t_wukv, 512, lambda s: vCc[:, s, :], [t_vCc], src=cn, tsrc=t_cn, nk=4, wfn=lambda k: wukv[:, k, :, :])
        if not last:
            dma_sp(knC_d[l].rearrange("j p t -> p j t")[:, :, tok0:tok0 + T], knCc[:], t_knCc[0], t_knCc, [t_kv[l][c]["knC"]])
            dma_sp(krC_d[l][:, tok0:tok0 + T], krCc[:], t_krCc, [t_krCc], [t_kv[l][c]["krC"]])
            dma_sp(vC_d[l][tok0:tok0 + T, :].rearrange("(s p) d -> p s d", p=128), vCc[:], t_vCc, [t_vCc], [t_kv[l][c]["vC"]])
        flush_rope()

    def phaseC_attn(l, c, bgl):
        bgl = list(bgl)
        sc.marks.append(("Ca", l, c, len(sc.streams["pe"])))
        npk = c * 4
        if c > 0:
            dma_sp(krpre[:, 0:c * T], krC_d[l][:, 0:c * T], t_krpre, [t_kv[l][cc]["krC"] for cc in range(c)], [t_krpre])
        for j in range(4):
            sl = j % 2
            rows = slice((j % 2) * 64, (j % 2) * 64 + 64)
            if c > 0:
                dma_sp(kpre[sl][:, 0:c * T], knC_d[l][j, :, 0:c * T], t_kpre[sl], [t_kv[l][cc]["knC"] for cc in range(c)], [t_kpre[sl]])
                dma_sp(vpre[sl][:, 0:npk, :], vC_d[l][0:c * T, j * 128:(j + 1) * 128].rearrange("(kb p) d -> p kb d", p=128),
                       t_vpre[sl], [t_kv[l][cc]["vC"] for cc in range(c)], [t_vpre[sl]])

            def kparts(kb, j=j, sl=sl, rows=rows):
                if kb < npk:
                    a = (kpre[sl][:, kb * 128:(kb + 1) * 128], [t_kpre[sl]])
                    b = (krpre[:, kb * 128:(kb + 1) * 128], [t_krpre])
                else:
                    kk = kb - npk
                    a = (knCc[:, j, kk * 128:(kk + 1) * 128], [t_knCc[j]])
                    b = (krCc[:, kk * 128:(kk + 1) * 128], [t_krCc])
                return [(a[0], lambda q0: Q[:, j, q0:T], a[1] + [t_Q[j]]),
                        (b[0], lambda q0: Q[:, 4 + j, q0:T], b[1] + [t_Q[4 + j]])]

            def vfn(kb, j=j, sl=sl):
                if kb < npk:
                    return vpre[sl][:, kb, :], [t_vpre[sl]]
                return vCc[:, kb - npk, j * 128:(j + 1) * 128], [t_vCc]

            def fin(j=j):
                r = acc_out()
                dve_rcp(rd[r][:], rd[r][:], [t_rd[r]], [t_rd[r]])
                dve_tt(df[:], t1[r][:], rd[r][:], ALU.mult, [t_t1[r], t_rd[r]], [t_df])
                dve_tt(yT[:, 12 + j, :], df[:], yT[:, 12 + j, :], ALU.mult, [t_df, t_yT[12 + j]], [t_yT[12 + j]])

            attn_head(kparts, vfn, 192.0 ** -0.5, c, npk, fin, bg=(bgl.pop(0) if bgl else None))
        while bgl:
            bgl.pop(0)()

    t_ost = [Tl("ost%d" % i) for i in range(NSB)]
    sto = sb("sto", [128, 64], F32); t_sto = Tl("sto")

    def phaseO(l, c):
        tok0 = c * T
        dst = out_d if l == depth - 1 else x1_d
        src = x_d if l == 0 else x1_d

        xparts = [None, None,
                  [(t1[0], t_t1[0]), (t1[1], t_t1[1]), (t2[0], t_t2[0]), (t2[1], t_t2[1])],
                  [(rd[0], t_rd[0]), (rd[1], t_rd[1]), (o1, t_o1), (df, t_df)]]

        def xload(s):
            r0 = tok0 + s * 128
            if xparts[s] is None:
                dma_sp(XR[s % 2][:], src[r0:r0 + 128, :], t_XR[s % 2], [t_x1[c]] if l > 0 else [], [t_XR[s % 2]])
            else:
                for q_, (tl_ap, tl_t) in enumerate(xparts[s]):
                    dma_sp(tl_ap[:], src[r0:r0 + 128, q_ * 512:(q_ + 1) * 512], tl_t, [t_x1[c]] if l > 0 else [], [tl_t])

        for s_ in range(NSB):
            xload(s_)
        for n in range(4):
            i0 = w_next_pair()
            s0 = wslot[i0]
            assert s0 % 2 == 0 and wslot[i0 + 1] == s0 + 1
            tws = [t_W[s0], t_W[s0 + 1]]
            for s in range(NSB):
                gi = nxt("g", 2)
                for k in range(KC):
                    mm(G[gi][:, :], yT[:, k, s * 128:(s + 1) * 128], WALL[:, s0:s0 + 2, k, :], k == 0, k == KC - 1,
                       tws + [t_yT[k]], [t_G[gi]])
                obc = OB[:, s, n * 512:(n + 1) * 512]
                act(obc, G[gi][:, :], AF.Copy, [t_G[gi]], t_OB[s])
                sc.op("act", lambda e, obc=obc, s=s, n=n: e.activation(out=hb[:, 0:512], in_=obc, func=AF.Square,
                                                                      accum_out=sto[:, s * 8 + n:s * 8 + n + 1]),
                      reads=t_OB[s], writes=[t_hb, t_sto])
                dve_tt(obc, obc, postw[:, n * 512:(n + 1) * 512], ALU.mult, t_OB[s] + [t_postw], t_OB[s])
        for s in range(NSB):
            r0 = tok0 + s * 128
            xr, txr = XR[s % 2], t_XR[s % 2]
            sc.op("dve", lambda e, s=s: e.reduce_sum(out=sto[:, 32 + s:33 + s], in_=sto[:, s * 8:s * 8 + 4], axis=mybir.AxisListType.X),
                  reads=[t_sto], writes=[t_sto])
            act(sto[:, 36 + s:37 + s], sto[:, 32 + s:33 + s], AF.Sqrt, [t_sto], [t_sto], scale=1.0 / D, bias=1e-6)
            dve_rcp(sto[:, 40 + s:41 + s], sto[:, 36 + s:37 + s], [t_sto], [t_sto])
            if xparts[s] is None:
                dve_stt(OB[:, s, :], OB[:, s, :], sto[:, 40 + s:41 + s], xr[:], ALU.mult, ALU.add, t_OB[s] + [t_sto, txr], t_OB[s])
            else:
                for q_, (tl_ap, tl_t) in enumerate(xparts[s]):
                    obq = OB[:, s, q_ * 512:(q_ + 1) * 512]
                    dve_stt(obq, obq, sto[:, 40 + s:41 + s], tl_ap[:], ALU.mult, ALU.add, t_OB[s] + [t_sto, tl_t], t_OB[s])
            dma_sp(dst[r0:r0 + 128, :], OB[:, s, :], t_ost[s], t_OB[s], [t_x1[c]] if l < depth - 1 else [], final=(l == depth - 1))

    order = [(l, c) for l in range(depth) for c in range(nb)]
    phase0(0, 0)
    for idx, (l, c) in enumerate(order):
        if c == 0:
            layer_setup(l)
        sc.marks.append(("A", l, c, len(sc.streams["pe"])))
        phaseA(l, c)
        sc.marks.append(("B", l, c, len(sc.streams["pe"])))
        phaseB(l, c)
        sc.marks.append(("C", l, c, len(sc.streams["pe"])))
        phaseC_in(l, c)
        bgl = phase0_bg(*order[idx + 1]) if idx + 1 < len(order) else []
        phaseC_attn(l, c, bgl)
        sc.marks.append(("O", l, c, len(sc.streams["pe"])))
        phaseO(l, c)

    sc.finalize()
    if os.environ.get("KVERB"):
        print("NOPS", sc.nops, {e: len(v) for e, v in sc.streams.items()})
    if os.environ.get("KMARKS"):
        import json as _json
        _json.dump(sc.marks, open(os.environ["KMARKS"], "w"))
    with nc.Block() as block:
        @block.tensor
        def _(e):
            sc.emit("pe", e)

        @block.scalar
        def _(e):
            sc.emit("act", e)

        @block.vector
        def _(e):
            sc.emit("dve", e)

        @block.gpsimd
        def _(e):
            sc.emit("pool", e)

        @block.sync
        def _(e):
            sc.emit("sp", e)
    es.close()
    return nc


def host_consts(inp):
    cst = np.zeros((128, NCST), np.float32)
    p = np.arange(128)
    for l in range(DEPTH):
        cst[:, C_PREW + l * 16:C_PREW + (l + 1) * 16] = np.asarray(inp["pre_norm_w"][l]).reshape(16, 128).T
        cst[:, C_QNW + l * 4:C_QNW + (l + 1) * 4] = np.asarray(inp["mla_q_norm_w"][l]).reshape(4, 128).T
        cst[:, C_KVNW + l * 4:C_KVNW + (l + 1) * 4] = np.asarray(inp["mla_kv_norm_w"][l]).reshape(4, 128).T
        cst[:, C_SUBLN + l] = np.asarray(inp["diff_subln_w"][l])
        sk = np.asarray(inp["sink_logits"][l])
        for i in range(8):
            cst[:, C_SINK + l * 8 + i] = sk[2 * i + p // 64]
        for v, nm in enumerate(("diff_lambda_q1", "diff_lambda_k1", "diff_lambda_q2", "diff_lambda_k2")):
            cst[:, C_LAM + l * 256 + v * 64:C_LAM + l * 256 + (v + 1) * 64] = np.asarray(inp[nm][l])[None, :]
    cst[:, C_INVF] = (10000.0 ** (-((p % 32) * 2) / 64.0)).astype(np.float32)
    cmat = np.zeros((128, 5, 128), np.float32)
    cmat[:, 0, :] = np.eye(128)
    R = np.zeros((128, 128), np.float32)
    for f in range(128):
        if f % 64 < 32:
            R[f, f + 32] = -1.0
        else:
            R[f, f - 32] = 1.0
    cmat[:, 1, :] = R.T
    k = np.arange(128)[:, None]
    q = np.arange(128)[None, :]
    cmat[:, 2, :] = (q >= k)
    cmat[:, 3, :] = (k > q)
    cmat[:, 4, :] = 1.0
    postw = np.ascontiguousarray(np.broadcast_to(np.asarray(inp["post_norm_w"])[:, None, :], (DEPTH, 128, D))).astype(np.float32)
    return cst, cmat, postw


_NC_CACHE = {}


def kernel(**inp):
    x = np.asarray(inp["x"], np.float32)
    pos = np.asarray(inp["positions"], np.int32)
    B = x.shape[0]
    cst, cmat, postw = host_consts(inp)
    w_in = np.ascontiguousarray(np.asarray(inp["w_in"], np.float32))
    w_out = np.ascontiguousarray(np.asarray(inp["w_out"], np.float32))
    w_uq = np.ascontiguousarray(np.asarray(inp["w_uq"], np.float32))
    w_ukv = np.ascontiguousarray(np.asarray(inp["w_ukv"], np.float32))
    if "nc" not in _NC_CACHE:
        _NC_CACHE["nc"] = build_nc()
    nc = _NC_CACHE["nc"]
    in_maps = []
    for core in range(NCORES):
        b = core % B
        in_maps.append({
            "x": np.ascontiguousarray(x[b]),
            "pos": np.ascontiguousarray(np.broadcast_to(pos[b][None, :], (128, S))),
            "cst": cst, "cmat": cmat, "postw": postw,
            "w_in": w_in, "w_out": w_out, "w_uq": w_uq, "w_ukv": w_ukv,
        })
    res = run_bass_kernel_spmd(nc, in_maps, core_ids=list(range(NCORES)))
    out = np.stack([np.asarray(res.results[b]["out"], np.float32) for b in range(B)], axis=0)
    return out
```

```python
import math
import os
from contextlib import ExitStack

import numpy as np
import concourse.bass as bass
import concourse.mybir as mybir
from concourse.bass_utils import run_bass_kernel_spmd

F32 = mybir.dt.float32
BF16 = mybir.dt.bfloat16
I32 = mybir.dt.int32
AF = mybir.ActivationFunctionType
ALU = mybir.AluOpType
PI = math.pi

D = 2048
S = 2048
DIN = 5952
KC = 16
T = 512
NSB = 4
DEPTH = 2
NCORES = 8
SAME_ENG_SYNC = True

C_PREW = 0
C_QNW = 32
C_KVNW = 40
C_SUBLN = 48
C_SINK = 50
C_INVF = 66
C_LAM = 67
NCST = C_LAM + 512


class Tl:
    __slots__ = ("name", "w", "r", "dsem", "dcnt", "excl")

    def __init__(self, name, excl=False):
        self.name = name
        self.excl = excl
        self.w = {}
        self.r = {}
        self.dsem = None
        self.dcnt = 0


class Ins:
    __slots__ = ("eng", "fn", "deps", "sig", "val", "sem", "isdma")

    def __init__(self, eng, fn, isdma=False):
        self.eng = eng
        self.fn = fn
        self.deps = ()
        self.sig = False
        self.val = 0
        self.sem = None
        self.isdma = isdma


class Sched:
    ENGS = ("pe", "act", "dve", "pool", "sp")

    def __init__(self, nc, es):
        self.nc = nc
        self.es = es
        self.streams = {e: [] for e in self.ENGS}
        self.esem = {e: es.enter_context(nc.semaphore("sem_" + e)) for e in ("pe", "act", "dve", "pool")}
        self.nsem = 0
        self.final = []
        import os
        self.nops = 0
        self.lo = int(os.environ.get("KLO", "1")) if os.environ.get("KLO") else 1 << 60
        self.hi = int(os.environ.get("KHI", "0"))
        self.marks = []
        self.maxops = int(os.environ.get("KMAXOPS", "100000000"))

    def _deps(self, ins, key, reads, writes):
        ex = [t for t in reads if t.excl]
        if ex:
            reads = [t for t in reads if not t.excl]
            writes = list(writes) + ex
        deps = set()
        for t in reads:
            deps.update(t.w.values())
        for t in writes:
            deps.update(t.w.values())
            deps.update(t.r.values())
        deps.discard(ins)
        ins.deps = tuple(deps)
        for t in reads:
            t.r[key] = ins
        for t in writes:
            t.w[key] = ins
            t.r = {}

    def op(self, eng, fn, reads=(), writes=()):
        self.nops += 1
        if self.lo <= self.nops <= self.hi:
            print("OP", self.nops, eng, [t.name for t in reads], [t.name for t in writes])
        if self.nops > self.maxops:
            return None
        ins = Ins(eng, fn)
        self._deps(ins, eng, reads, writes)
        self.streams[eng].append(ins)
        return ins

    def dma(self, q, fn, sb, reads=(), writes=(), final=False):
        self.nops += 1
        if self.nops > self.maxops:
            return None
        ins = Ins(q, fn, isdma=True)
        if sb.dsem is None:
            sb.dsem = self.es.enter_context(self.nc.semaphore("ds%d" % self.nsem))
            self.nsem += 1
        self._deps(ins, ("d", id(sb)), reads, writes)
        sb.dcnt += 16
        ins.sem = sb.dsem
        ins.val = sb.dcnt
        self.streams[q].append(ins)
        if final:
            self.final.append(ins)
        return ins

    def finalize(self):
        for st in self.streams.values():
            for ins in st:
                for d in ins.deps:
                    d.sig = True
        for e, st in self.streams.items():
            cnt = 0
            for ins in st:
                if ins.isdma:
                    continue
                if ins.sig:
                    cnt += 1
                    ins.val = cnt
                    ins.sem = self.esem[e]

    def emit(self, e, eng):
        waited = {}
        for ins in self.streams[e]:
            need = {}
            for d in ins.deps:
                if (not d.isdma) and d.eng == e and (e == "pe" or not SAME_ENG_SYNC):
                    continue
                k = id(d.sem)
                if k not in need or need[k][1] < d.val:
                    need[k] = (d.sem, d.val)
            for k, (sem, v) in need.items():
                if waited.get(k, 0) < v:
                    eng.wait_ge(sem, v)
                    waited[k] = v
            r = ins.fn(eng)
            if ins.isdma:
                r.then_inc(ins.sem, 16)
            elif ins.sig:
                r.then_inc(ins.sem, 1)
        if e == "sp":
            for ins in self.final:
                k = id(ins.sem)
                if waited.get(k, 0) < ins.val:
                    eng.wait_ge(ins.sem, ins.val)
                    waited[k] = ins.val


def build_nc(nb=4, depth=DEPTH, dbg=False):
    nc = bass.Bass("TRN2", target_bir_lowering=False)
    es = ExitStack()
    ntok = nb * T

    def din(name, shape, dt):
        return nc.dram_tensor(name, shape, dt, kind="ExternalInput").ap()

    x_d = din("x", [S, D], F32)
    pos_d = din("pos", [128, S], I32)
    cst_d = din("cst", [128, NCST], F32)
    cmat_d = din("cmat", [128, 5, 128], F32)
    postw_d = din("postw", [DEPTH, 128, D], F32)
    win_d = din("w_in", [DEPTH, D, DIN], F32)
    wout_d = din("w_out", [DEPTH, D, D], F32)
    wuq_d = din("w_uq", [DEPTH, 512, 768], F32)
    wukv_d = din("w_ukv", [DEPTH, 512, 1024], F32)
    out_d = nc.dram_tensor("out", [S, D], F32, kind="ExternalOutput").ap()
    dbg_d = nc.dram_tensor("dbg", [8, 128, 2048], F32, kind="ExternalOutput").ap() if dbg else None

    def dint(name, shape, dt):
        return nc.dram_tensor(name, shape, dt, kind="Internal").ap()

    cs_d = dint("cs_d", [2, 128, S], F32)
    x1_d = dint("x1_d", [S, D], F32)
    kA_d = [dint("kA_d%d" % l, [4, 128, S], BF16) for l in range(depth)]
    vA_d = [dint("vA_d%d" % l, [S, 512], BF16) for l in range(depth)]
    knC_d = [dint("knC_d%d" % l, [4, 128, S], BF16) for l in range(depth)]
    krC_d = [dint("krC_d%d" % l, [128, S], BF16) for l in range(depth)]
    vC_d = [dint("vC_d%d" % l, [S, 512], BF16) for l in range(depth)]

    sc = Sched(nc, es)

    def sb(name, shape, dt):
        return es.enter_context(nc.sbuf_tensor("sb_" + name, shape, dt))

    def ps(name, shape, dt):
        return es.enter_context(nc.psum_tensor("ps_" + name, shape, dt))

    cst = sb("cst", [128, NCST], F32); t_cst = Tl("cst")
    cmb = sb("cmb", [128, 5, 128], BF16); t_cmb = Tl("cmb")
    ident, RT, tri, atri, ones = (cmb[:, i, :] for i in range(5))
    negb = sb("negb", [128, 256], BF16); t_negb = Tl("negb")
    postw = sb("postw", [128, D], F32); t_postw = Tl("postw")
    cs = sb("cs", [128, 2, T], F32); t_cs = Tl("cs")
    XR = [sb("xr%d" % i, [128, D], F32) for i in range(2)]; t_XR = [Tl("xr%d" % i) for i in range(2)]
    hb = sb("hb", [128, D], BF16); t_hb = Tl("hb")
    st = sb("st", [128, 32], F32); t_st = Tl("st")
    lam = sb("lam", [128, 8], F32); t_lam = Tl("lam")
    lamt = sb("lamt", [128, 4, 64], F32)
    esink = sb("esink", [128, 16], F32); t_esink = Tl("esink")

    BIG = sb("big", [128, 4608], F32)
    bigb = BIG[:].bitcast(BF16)
    hT = bigb[:, 0:8192].rearrange("p (k t) -> p k t", k=16); t_hT = Tl("hT")
    qraw = [bigb[:, 8192 + i * 512:8192 + (i + 1) * 512] for i in range(2)]; t_qraw = [Tl("qraw%d" % i) for i in range(2)]
    BIG2 = sb("big2", [128, 8192], F32)
    OB = BIG2[:].rearrange("p (s d) -> p s d", s=4)
    b2b = BIG2[:].bitcast(BF16)
    cf = BIG2[:, 0:2048].rearrange("p (k t) -> p k t", k=4); t_cf = Tl("cf")
    cn = b2b[:, 4096:6144].rearrange("p (k t) -> p k t", k=4); t_cn = Tl("cn")
    sq = [b2b[:, 6144 + i * 512:6144 + (i + 1) * 512] for i in range(2)]; t_sq = [Tl("sq%d" % i) for i in range(2)]
    npre = max(128, (nb - 1) * T)
    assert npre <= 1536
    NE = 4
    E = [b2b[:, 7168:7680], b2b[:, 7680:8192], b2b[:, 11264:11776], sb("E3", [128, T], BF16)[:]]
    t_E = [Tl("E%d" % i) for i in range(NE)]
    kpre = [b2b[:, 8192 + i * 1536:8192 + i * 1536 + npre] for i in range(2)]; t_kpre = [Tl("kpre%d" % i) for i in range(2)]
    vpre = [b2b[:, 12288 + i * 1536:12288 + i * 1536 + npre].rearrange("p (kb d) -> p kb d", d=128) for i in range(2)]
    t_vpre = [Tl("vpre%d" % i) for i in range(2)]
    krpre = sb("krpre", [128, npre], BF16)[:]; t_krpre = Tl("krpre")
    t_OB = [[t_cf],
            [t_cn, t_sq[0], t_sq[1], t_E[0], t_E[1]],
            [t_kpre[0], t_kpre[1], t_E[2]],
            [t_vpre[0], t_vpre[1]]]
    t_big2 = [t_cf, t_cn] + t_sq + t_kpre + t_vpre + t_E[0:3]

    Q = sb("Q", [128, 8, T], BF16); t_Q = [Tl("Q%d" % i) for i in range(8)]
    yT = sb("yT", [128, 16, T], BF16); t_yT = [Tl("yT%d" % i) for i in range(16)]
    kAc = sb("kAc", [128, 4, T], BF16); t_kAc = [Tl("kAc%d" % i) for i in range(4)]
    vAc = sb("vAc", [128, 4, 512], BF16); t_vAc = Tl("vAc")
    knCc = sb("knCc", [128, 4, T], BF16); t_knCc = [Tl("knCc%d" % i) for i in range(4)]
    krCc = sb("krCc", [128, T], BF16); t_krCc = Tl("krCc")
    vCc = sb("vCc", [128, 4, 512], BF16); t_vCc = Tl("vCc")
    kBz = [[sb("kBz%d%d" % (g_, r_), [128, 640], BF16) for r_ in range(2)] for g_ in range(2)]; t_kB = Tl("kB")
    vB = sb("vB", [128, 16, 128], BF16); t_vB = Tl("vB")
    NW = 4
    WALL = sb("wall", [128, NW, KC, 256], BF16)
    W = [WALL[:, i] for i in range(NW)]; t_W = [Tl("W%d" % i) for i in range(NW)]
    wuqn = sb("wuqn", [128, 4, 4, 128], BF16); t_wuqn = Tl("wuqn")
    wuqr = sb("wuqr", [128, 4, 4, 64], BF16); t_wuqr = Tl("wuqr")
    wukk = sb("wukk", [128, 4, 4, 128], BF16); t_wukk = Tl("wukk")
    wukv = sb("wukv", [128, 4, 4, 128], BF16); t_wukv = Tl("wukv")
    t1 = [sb("t1_%d" % i, [128, T], F32) for i in range(2)]; t_t1 = [Tl("t1_%d" % i) for i in range(2)]
    t2 = [sb("t2_%d" % i, [128, T], F32) for i in range(2)]; t_t2 = [Tl("t2_%d" % i) for i in range(2)]
    rd = [sb("rd%d" % i, [128, T], F32) for i in range(2)]; t_rd = [Tl("rd%d" % i) for i in range(2)]
    o1 = sb("o1", [128, T], F32); t_o1 = Tl("o1")
    df = sb("df", [128, T], F32); t_df = Tl("df")
    nrm = sb("nrm", [128, T], F32); t_nrm = Tl("nrm")

    G = [ps("g%d" % i, [128, 512], F32) for i in range(3)]; t_G = [Tl("g%d" % i, True) for i in range(3)]
    NS_ = 3
    SP_ = [ps("s%d" % i, [128, 512], F32) for i in range(NS_)]; t_S = [Tl("s%d" % i, True) for i in range(NS_)]
    ao = ps("ao", [128, 512], F32); t_ao = Tl("ao", True)
    ad = ps("ad", [128, 512], F32); t_ad = Tl("ad", True)

    t_csd = Tl("cs_d")
    t_x1 = [Tl("x1_%d" % c) for c in range(nb)]
    t_kv = [[{n: Tl("%s_%d_%d" % (n, l, c)) for n in ("kA", "vA", "knC", "krC", "vC")} for c in range(nb)] for l in range(depth)]

    cnt = {"g": 0, "e": 0, "s": 0, "rp": 0, "rd": 0, "w": 0}
    PDEPTH = 2
    pending_rope = []

    def flush_rope():
        while pending_rope:
            pending_rope.pop(0)()

    def nxt(k, n):
        v = cnt[k] % n
        cnt[k] += 1
        return v

    def mm(out, lhsT, rhs, start, stop, reads, writes, skip=False):
        if skip:
            sc.op("pe", lambda e: e.matmul(out, lhsT=lhsT, rhs=rhs, start=start, stop=stop, skip_group_check=True),
                  reads=reads, writes=writes)
        else:
            sc.op("pe", lambda e: e.matmul(out, lhsT=lhsT, rhs=rhs, start=start, stop=stop), reads=reads, writes=writes)

    def act(out, in_, func, reads, writes, **kw):
        sc.op("act", lambda e: e.activation(out=out, in_=in_, func=func, **kw), reads=reads, writes=writes)

    def dve_tt(out, in0, in1, op, reads, writes):
        sc.op("dve", lambda e: e.tensor_tensor(out=out, in0=in0, in1=in1, op=op), reads=reads, writes=writes)

    def dve_ts(out, in0, s1, s2, op0, op1, reads, writes):
        if op1 is None:
            sc.op("dve", lambda e: e.tensor_scalar(out=out, in0=in0, scalar1=s1, scalar2=None, op0=op0), reads=reads, writes=writes)
        else:
            sc.op("dve", lambda e: e.tensor_scalar(out=out, in0=in0, scalar1=s1, scalar2=s2, op0=op0, op1=op1), reads=reads, writes=writes)

    def dve_stt(out, in0, scalar, in1, op0, op1, reads, writes):
        sc.op("dve", lambda e: e.scalar_tensor_tensor(out=out, in0=in0, scalar=scalar, in1=in1, op0=op0, op1=op1), reads=reads, writes=writes)

    def pool_tt(out, in0, in1, op, reads, writes):
        sc.op("pool", lambda e: e.tensor_tensor(out=out, in0=in0, in1=in1, op=op), reads=reads, writes=writes)

    late_ops = []

    def flush_late():
        while late_ops:
            late_ops.pop(0)()

    def dve_rcp(out, in_, reads, writes):
        sc.op("dve", lambda e: e.reciprocal(out=out, in_=in_), reads=reads, writes=writes)

    def dve_cp(out, in_, reads, writes):
        sc.op("dve", lambda e: e.tensor_copy(out=out, in_=in_), reads=reads, writes=writes)

    def dma_sp(out, in_, sbt, reads, writes, final=False):
        sc.dma("sp", lambda e: e.dma_start(out=out, in_=in_), sbt, reads=reads, writes=writes, final=final)

    def dma_pool(out, in_, sbt, reads, writes):
        sc.dma("pool", lambda e: e.dma_start(out=out, in_=in_), sbt, reads=reads, writes=writes)

    dbg_n = [0]

    def dump(ap, tls, rows=128, cols=None):
        if not dbg or dbg_n[0] >= 8:
            return
        i = dbg_n[0]
        dbg_n[0] += 1
        cols = cols or ap.shape[-1]
        tmpt = sb("dbgt%d" % i, [128, cols], F32)
        tt = Tl("dbgt%d" % i)
        dve_cp(tmpt[0:rows, :], ap, tls, [tt])
        dma_sp(dbg_d[i, 0:rows, 0:cols], tmpt[0:rows, :], tt, [tt], [], final=True)
        return i

    wq = []

    def wq_add(parts):
        wq.append(parts)
        return len(wq) - 1

    win_v = [win_d[l].rearrange("(k p) c -> p k c", p=128) for l in range(DEPTH)]
    wout_v = [wout_d[l].rearrange("(k p) c -> p k c", p=128) for l in range(DEPTH)]

    def block_chunks(l):
        ch = []
        for c0 in range(0, 2048, 256):
            ch.append([(0, 256, win_v[l][:, :, c0:c0 + 256])])
        for c0 in range(2048, 3072, 256):
            ch.append([(0, 256, win_v[l][:, :, c0:c0 + 256])])
        ch.append([(0, 64, win_v[l][:, :, 3072:3136]), (64, 64, win_v[l][:, :, 3072:3136]),
                   (128, 64, win_v[l][:, :, 3136:3200]), (192, 64, win_v[l][:, :, 3136:3200])])
        ch.append([(0, 128, win_v[l][:, :, 3200:3328])])
        for c0 in range(3328, 4352, 256):
            ch.append([(0, 256, win_v[l][:, :, c0:c0 + 256])])
        for c0 in range(4352, 4864, 256):
            ch.append([(0, 256, win_v[l][:, :, c0:c0 + 256])])
        ch.append([(0, 64, win_v[l][:, :, 5376:5440]), (64, 64, win_v[l][:, :, 5376:5440])])
        for c0 in range(5440, 5952, 256):
            ch.append([(0, 256, win_v[l][:, :, c0:c0 + 256])])
        for c0 in range(4864, 5376, 256):
            ch.append([(0, 256, win_v[l][:, :, c0:c0 + 256])])
        for c0 in range(0, 2048, 256):
            ch.append([(0, 256, wout_v[l][:, :, c0:c0 + 256]), "pair%d" % ((c0 // 256) % 2)])
        return ch

    wslot = []
    _sc = 0
    for l in range(depth):
        for c in range(nb):
            for parts in block_chunks(l):
                tag = None
                if isinstance(parts[-1], str):
                    tag = parts[-1]
                    parts = parts[:-1]
                if tag == "pair0" and _sc % 2 == 1:
                    _sc += 1
                wq_add(parts)
                wslot.append(_sc % NW)
                _sc += 1
    wstate = {"issued": 0, "cur": -1}

    def w_issue_upto(i):
        while wstate["issued"] <= min(i, len(wq) - 1):
            j = wstate["issued"]
            slot = wslot[j]
            if any(wslot[q] == slot for q in range(max(wstate.get("protect", 0), 0), j)):
                break
            for (d0, n, src) in wq[j]:
                dma_pool(W[slot][:, :, d0:d0 + n], src, t_W[slot], [], [t_W[slot]])
            wstate["issued"] += 1

    def w_next():
        wstate["cur"] += 1
        i = wstate["cur"]
        wstate["protect"] = i
        w_issue_upto(i + NW - 1)
        return W[wslot[i]], t_W[wslot[i]]

    def w_next_pair():
        wstate["cur"] += 2
        i0 = wstate["cur"] - 1
        wstate["protect"] = i0
        w_issue_upto(i0 + NW - 1)
        return i0

    dma_sp(cst[:], cst_d, t_cst, [], [t_cst])
    dma_pool(cmb[:], cmat_d, t_cmb, [], [t_cmb])
    w_issue_upto(NW - 2)
    for g_ in range(2):
        for r_ in range(2):
            sc.op("dve", lambda e, g_=g_, r_=r_: e.memset(kBz[g_][r_][:], 0.0), reads=[], writes=[t_kB])
    dve_ts(negb[:, 0:128], tri, -1.0, 30000.0, ALU.add, ALU.mult, [t_cmb], [t_negb])
    dve_ts(negb[:, 128:256], atri, -1.0, 30000.0, ALU.add, ALU.mult, [t_cmb], [t_negb])
    pi_t = XR[0][:].bitcast(I32)
    dma_sp(pi_t, pos_d, t_XR[0], [], [t_XR[0]])
    pf = XR[1]
    kf = OB[:, 0, :]
    ki = OB[:, 1, :].bitcast(I32)
    sa = OB[:, 2, :]
    ca = OB[:, 3, :]
    tb = t_big2
    dve_cp(pf[:], pi_t, [t_XR[0]], [t_XR[1]])
    dve_ts(pf[:], pf[:], cst[:, C_INVF:C_INVF + 1], None, ALU.mult, None, [t_XR[1], t_cst], [t_XR[1]])
    dve_ts(kf, pf[:], 1.0 / (2 * PI), None, ALU.mult, None, [t_XR[1]], tb)
    dve_cp(ki, kf, tb, tb)
    dve_cp(kf, ki, tb, tb)
    c1 = 6.28125
    c2 = float(np.float32(2 * PI - c1))
    dve_stt(pf[:], kf, -c1, pf[:], ALU.mult, ALU.add, tb + [t_XR[1]], [t_XR[1]])
    dve_stt(pf[:], kf, -c2, pf[:], ALU.mult, ALU.add, tb + [t_XR[1]], [t_XR[1]])
    dve_ts(kf, pf[:], PI, -2 * PI, ALU.is_gt, ALU.mult, [t_XR[1]], tb)
    dve_tt(sa, pf[:], kf, ALU.add, [t_XR[1]] + tb, tb)
    dve_ts(pf[:], pf[:], PI / 2, None, ALU.add, None, [t_XR[1]], [t_XR[1]])
    dve_ts(kf, pf[:], PI, -2 * PI, ALU.is_gt, ALU.mult, [t_XR[1]], tb)
    dve_tt(ca, pf[:], kf, ALU.add, [t_XR[1]] + tb, tb)
    act(OB[:, 2:4, :], OB[:, 2:4, :], AF.Sin, tb, tb)
    dma_sp(cs_d[0], ca, t_cf, tb, [t_csd])
    dma_sp(cs_d[1], sa, t_cf, tb, [t_csd])

    def layer_setup(l):
        dma_sp(postw[:], postw_d[l], t_postw, [], [t_postw])
        wq_v = wuq_d[l].rearrange("(k p) (h d) -> p k h d", p=128, d=192)
        wk_v = wukv_d[l].rearrange("(k p) (h d) -> p k h d", p=128, d=256)
        for k in range(4):
            dma_pool(wuqn[:, k], wq_v[:, k, :, 0:128], t_wuqn, [], [t_wuqn])
            dma_pool(wuqr[:, k], wq_v[:, k, :, 128:192], t_wuqr, [], [t_wuqr])
            dma_pool(wukk[:, k], wk_v[:, k, :, 0:128], t_wukk, [], [t_wukk])
            dma_pool(wukv[:, k], wk_v[:, k, :, 128:256], t_wukv, [], [t_wukv])
        lam_init = 0.8 - 0.6 * math.exp(-0.3 * l)
        base = C_LAM + l * 256
        lv = cst[:, base:base + 256].rearrange("p (v i) -> p v i", v=4)
        dve_tt(lamt[:, 0, :], lv[:, 0, :], lv[:, 1, :], ALU.mult, [t_cst], [t_lam])
        dve_tt(lamt[:, 1, :], lv[:, 2, :], lv[:, 3, :], ALU.mult, [t_cst], [t_lam])
        sc.op("dve", lambda e: e.reduce_sum(out=lam[:, 2:4], in_=lamt[:, 0:2, :], axis=mybir.AxisListType.X), reads=[t_lam], writes=[t_lam])
        act(lam[:, 4:6], lam[:, 2:4], AF.Exp, [t_lam], [t_lam])
        dve_stt(lam[:, 0:1], lam[:, 5:6], -lam_init, lam[:, 4:5], ALU.add, ALU.subtract, [t_lam], [t_lam])
        dve_ts(lam[:, 1:2], cst[:, C_SUBLN + l:C_SUBLN + l + 1], 1.0 - lam_init, None, ALU.mult, None, [t_cst, t_lam], [t_lam])
        act(esink[:, 0:8], cst[:, C_SINK + l * 8:C_SINK + l * 8 + 8], AF.Exp, [t_cst], [t_esink])

    def p0_cs(l, c):
        tok0 = c * T
        dma_sp(cs[:], cs_d[:, :, tok0:tok0 + T].rearrange("a p t -> p a t"), t_cs, [t_csd], [t_cs])

    def p0_stats(l, c, s):
        tok0 = c * T
        src = x_d if l == 0 else x1_d
        xr, txr = XR[s % 2], t_XR[s % 2]
        r0 = tok0 + s * 128
        dma_sp(xr[:], src[r0:r0 + 128, :], txr, [t_x1[c]] if l > 0 else [], [txr])
        sc.op("act", lambda e, xr=xr, s=s: e.activation(out=hb[:], in_=xr[:], func=AF.Square, accum_out=st[:, s:s + 1]),
              reads=[txr], writes=[t_hb, t_st])
        act(st[:, 4 + s:5 + s], st[:, s:s + 1], AF.Sqrt, [t_st], [t_st], scale=1.0 / D, bias=1e-6)
        dve_rcp(st[:, 8 + s:9 + s], st[:, 4 + s:5 + s], [t_st], [t_st])
        dve_ts(hb[:], xr[:], st[:, 8 + s:9 + s], None, ALU.mult, None, [txr, t_st], [t_hb])

    def p0_tr(l, c, s):
        for g in range(4):
            gi = nxt("g", 2)
            tp = G[gi][:].bitcast(BF16).rearrange("p (j t) -> p j t", j=8)[:, 0:4, :]
            for j in range(4):
                k = g * 4 + j
                sc.op("pe", lambda e, tp=tp, j=j, k=k: e.transpose(out=tp[:, j, :], in_=hb[:, k * 128:(k + 1) * 128], identity=ident),
                      reads=[t_hb, t_cmb], writes=[t_G[gi]])
            pw = cst[:, C_PREW + l * 16 + g * 4:C_PREW + l * 16 + g * 4 + 4].unsqueeze(2).broadcast_to([128, 4, 128])
            dve_tt(hT[:, g * 4:(g + 1) * 4, s * 128:(s + 1) * 128], tp, pw, ALU.mult, [t_G[gi], t_cst], [t_hT])

    def phase0(l, c):
        p0_cs(l, c)
        for s in range(NSB):
            p0_stats(l, c, s)
            p0_tr(l, c, s)

    def phase0_bg(l, c):
        def mk(i):
            def f():
                if i == 0:
                    p0_cs(l, c)
                if i > 0:
                    p0_tr(l, c, i - 1)
                if i < NSB:
                    p0_stats(l, c, i)
            return f
        return [mk(i) for i in range(NSB + 1)]

    def fm_group(wsl, twsl, col0, m=128):
        gi = nxt("g", 2)
        for k in range(KC):
            mm(G[gi][0:m, :], wsl[:, k, col0:col0 + m], hT[:, k, :], k == 0, k == KC - 1, [twsl, t_hT], [t_G[gi]])
        flush_rope()
        flush_late()
        return gi

    def rope_evac(gi, dst, tdst):
        r = nxt("rp", 2)
        act(qraw[r], G[gi][:], AF.Copy, [t_G[gi]], [t_qraw[r]])
        dve_tt(t1[r][:], G[gi][:], cs[:, 0, :], ALU.mult, [t_G[gi], t_cs], [t_t1[r]])

        def stage_b(r=r, dst=dst, tdst=tdst):
            mm(G[2][:], RT, qraw[r], True, True, [t_cmb, t_qraw[r]], [t_G[2]])
            dve_tt(t2[r][:], G[2][:], cs[:, 1, :], ALU.mult, [t_G[2], t_cs], [t_t2[r]])
            if isinstance(dst, list):
                for (rw, ap) in dst:
                    dve_tt(ap, t1[r][rw, :], t2[r][rw, :], ALU.add, [t_t1[r], t_t2[r]], tdst)
            else:
                dve_tt(dst, t1[r][:], t2[r][:], ALU.add, [t_t1[r], t_t2[r]], tdst)
        pending_rope.append(stage_b)

    def tm_group(wsl, twsl, ncols, dst_fn, tdst, src=None, tsrc=None, nk=KC, wfn=None):
        flush_rope()
        for s in range(NSB):
            gi = nxt("g", 2)
            for k in range(nk):
                lhsT = (hT if src is None else src)[:, k, s * 128:(s + 1) * 128]
                rhs = wsl[:, k, 0:ncols] if wfn is None else wfn(k)
                mm(G[gi][:, 0:ncols], lhsT, rhs, k == 0, k == nk - 1, [twsl, t_hT if src is None else tsrc], [t_G[gi]])
            act(dst_fn(s), G[gi][:, 0:ncols], AF.Copy, [t_G[gi]], tdst)

    def attn_head(kparts, vfn, scale, c, nprefix_kb, finalize, bg=None):
        flush_rope()
        nkb = nprefix_kb + 4
        pend = []

        def stage2(kb, ei, q0):
            vl, vt = vfn(kb)
            mm(ao[:, q0:T], vl, E[ei][:, q0:T], kb == 0, kb == nkb - 1, vt + [t_E[ei]], [t_ao])
            mm(ad[:, q0:T], ones, E[ei][:, q0:T], kb == 0, kb == nkb - 1, [t_cmb, t_E[ei]], [t_ad])

        for kb in range(nkb):
            q0 = max(0, (kb - nprefix_kb) * 128)
            si = nxt("s", NS_)
            parts = kparts(kb)
            diag = kb >= nprefix_kb
            if diag:
                mm(SP_[si][:, q0:q0 + 128], ident, negb[:, 0:128], True, True, [t_cmb, t_negb], [t_S[si]])
            for pi_, (lh, rhf, tls) in enumerate(parts):
                mm(SP_[si][:, q0:T], lh, rhf(q0), (pi_ == 0) and not diag, pi_ == len(parts) - 1, tls, [t_S[si]], skip=diag)
            ei = nxt("e", NE)
            act(E[ei][:, q0:T], SP_[si][:, q0:T], AF.Exp, [t_S[si]], [t_E[ei]], scale=scale)
            pend.append((kb, ei, q0))
            if len(pend) > PDEPTH:
                stage2(*pend.pop(0))
            if kb == min(3, nkb - 1):
                flush_late()
            if bg is not None and kb == 1:
                bg()
        for p in pend:
            stage2(*p)
        finalize()

    def acc_out(use_dve=True):
        r = nxt("rd", 2)
        act(rd[r][:], ad[:], AF.Copy, [t_ad], [t_rd[r]])
        if use_dve:
            dve_cp(t1[r][:], ao[:], [t_ao], [t_t1[r]])
        else:
            act(t1[r][:], ao[:], AF.Copy, [t_ao], [t_t1[r]])
        return r

    def phaseA(l, c):
        tok0 = c * T
        last = (c == nb - 1)
        sc.op("pool", lambda e: e.memset(Q[64:128, 0:4, :], 0.0), reads=[], writes=t_Q[0:4])
        sc.op("pool", lambda e: e.memset(Q[0:64, 4:8, :], 0.0), reads=[], writes=t_Q[4:8])
        for i in range(2):
            wsl, tw = w_next()
            for h in range(2):
                gi = fm_group(wsl, tw, h * 128)
                jj = i * 2 + h
                rope_evac(gi, [(slice(0, 64), Q[0:64, jj, :]), (slice(64, 128), Q[64:128, 4 + jj, :])], [t_Q[jj], t_Q[4 + jj]])
        for i in range(2):
            wsl, tw = w_next()
            for h in range(2):
                gi = fm_group(wsl, tw, h * 128)
                rope_evac(gi, kAc[:, i * 2 + h, :], [t_kAc[i * 2 + h]])
        for i in range(2):
            wsl, tw = w_next()
            tm_group(wsl, tw, 256, lambda s, i=i: vAc[:, s, i * 256:(i + 1) * 256], [t_vAc])
        for i in range(2):
            wsl, tw = w_next()
            for h in range(2):
                gi = fm_group(wsl, tw, h * 128)
                act(yT[:, i * 2 + h, :], G[gi][:], AF.Silu, [t_G[gi]], [t_yT[i * 2 + h]])
        if not last:
            dma_sp(kA_d[l].rearrange("j p t -> p j t")[:, :, tok0:tok0 + T], kAc[:], t_kAc[0], t_kAc, [t_kv[l][c]["kA"]])
            dma_sp(vA_d[l][tok0:tok0 + T, :].rearrange("(s p) d -> p s d", p=128), vAc[:], t_vAc, [t_vAc], [t_kv[l][c]["vA"]])
        npk = c * 4
        sc.marks.append(("Aa", l, c, len(sc.streams["pe"])))
        for j in range(4):
            sl = j % 2
            if c > 0:
                dma_sp(kpre[sl][:, 0:c * T], kA_d[l][j, :, 0:c * T], t_kpre[sl], [t_kv[l][cc]["kA"] for cc in range(c)], [t_kpre[sl]])
                dma_sp(vpre[sl][:, 0:npk, :], vA_d[l][0:c * T, j * 128:(j + 1) * 128].rearrange("(kb p) d -> p kb d", p=128),
                       t_vpre[sl], [t_kv[l][cc]["vA"] for cc in range(c)], [t_vpre[sl]])
            for comp in range(2):
                rows = slice(comp * 64, comp * 64 + 64)

                def kparts(kb, comp=comp, j=j, sl=sl):
                    if kb < npk:
                        lh = kpre[sl][:, kb * 128:(kb + 1) * 128]
                        tl = [t_kpre[sl]]
                    else:
                        kk = kb - npk
                        lh = kAc[:, j, kk * 128:(kk + 1) * 128]
                        tl = [t_kAc[j]]
                    qi = comp * 4 + j
                    return [(lh, lambda q0: Q[:, qi, q0:T], tl + [t_Q[qi]])]

                def vfn(kb, j=j, sl=sl):
                    if kb < npk:
                        return vpre[sl][:, kb, :], [t_vpre[sl]]
                    return vAc[:, kb - npk, j * 128:(j + 1) * 128], [t_vAc]

                def fin(comp=comp, j=j):
                    r = acc_out()
                    dve_rcp(rd[r][:], rd[r][:], [t_rd[r]], [t_rd[r]])
                    if comp == 0:
                        dve_tt(o1[:], t1[r][:], rd[r][:], ALU.mult, [t_t1[r], t_rd[r]], [t_o1])
                    else:
                        dve_tt(df[:], t1[r][:], rd[r][:], ALU.mult, [t_t1[r], t_rd[r]], [t_df])
                        dve_stt(df[:], df[:], lam[:, 0:1], o1[:], ALU.mult, ALU.add, [t_df, t_lam, t_o1], [t_df])
                        def late(j=j):
                            act(sq[0], df[:], AF.Square, [t_df], [t_sq[0]])
                            mm(G[2][:], ones, sq[0], True, True, [t_cmb, t_sq[0]], [t_G[2]])
                            act(nrm[:], G[2][:], AF.Sqrt, [t_G[2]], [t_nrm], scale=1.0 / 128, bias=1e-5)
                            dve_rcp(nrm[:], nrm[:], [t_nrm], [t_nrm])
                            dve_stt(df[:], df[:], lam[:, 1:2], nrm[:], ALU.mult, ALU.mult, [t_df, t_lam, t_nrm], [t_df])
                            dve_tt(yT[:, j, :], df[:], yT[:, j, :], ALU.mult, [t_df, t_yT[j]], [t_yT[j]])
                        late_ops.append(late)

                attn_head(kparts, vfn, 0.125, c, npk, fin)

    def phaseB(l, c):
        tok0 = c * T
        for i in range(4):
            wsl, tw = w_next()
            for h in range(2):
                gi = fm_group(wsl, tw, h * 128)
                rope_evac(gi, Q[:, i * 2 + h, :], [t_Q[i * 2 + h]])
        wsl, tw = w_next()
        for g in range(2):
            gi = fm_group(wsl, tw, g * 128)
            rope_evac(gi, [(slice(0, 64), kBz[g][0][0:64, 128:640]), (slice(64, 128), kBz[g][1][64:128, 128:640])], [t_kB])
        wsl, tw = w_next()
        tm_group(wsl, tw, 128, lambda s: vB[:, c * 4 + s, :], [t_vB])
        for i in range(4):
            wsl, tw = w_next()
            for h in range(2):
                gi = fm_group(wsl, tw, h * 128)
                act(yT[:, 4 + i * 2 + h, :], G[gi][:], AF.Silu, [t_G[gi]], [t_yT[4 + i * 2 + h]])
        flush_rope()
        sc.marks.append(("Ba", l, c, len(sc.streams["pe"])))
        for i in range(8):
            g = i // 4
            pend = []

            def stage2(kbg, ei, qb_lo, qb_hi, g=g):
                for r in range(2):
                    rows = slice(r * 64, r * 64 + 64)
                    for qb in range(qb_lo, qb_hi + 1):
                        qq = (qb - c * 4) * 128
                        ec = r * 256 + (qb - qb_lo) * 128
                        first = (kbg == qb - 1) or (qb == 0)
                        lastk = (kbg == qb)
                        mm(ao[rows, qq:qq + 128], vB[:, kbg, g * 64:(g + 1) * 64], E[ei][:, ec:ec + 128], first, lastk,
                           [t_vB, t_E[ei]], [t_ao])
                        mm(ad[rows, qq:qq + 128], ones[:, 0:64], E[ei][:, ec:ec + 128], first, lastk,
                           [t_cmb, t_E[ei]], [t_ad])

            for kbg in range(c * 4 - 1, c * 4 + 4):
                if kbg < 0:
                    continue
                qb_lo = max(kbg, c * 4)
                qb_hi = min(kbg + 1, c * 4 + 3)
                q0 = (qb_lo - c * 4) * 128
                q1 = (qb_hi - c * 4 + 1) * 128
                N = q1 - q0
                si = nxt("s", NS_)
                kc0 = (kbg - c * 4) * 128 + 128
                if N == 256:
                    nsel = negb[:, 0:256]
                elif qb_lo == kbg:
                    nsel = negb[:, 0:128]
                else:
                    nsel = negb[:, 128:256]
                for r in range(2):
                    mm(SP_[si][:, r * 256:r * 256 + N], ident, nsel, True, False, [t_cmb, t_negb], [t_S[si]])
                    mm(SP_[si][:, r * 256:r * 256 + N], kBz[g][r][:, kc0:kc0 + 128], Q[:, i, q0:q1], False, True,
                       [t_kB, t_Q[i]], [t_S[si]])
                ei = nxt("e", NE)
                sv = SP_[si][:].rearrange("p (r n) -> p r n", r=2)[:, :, 0:N]
                ev = E[ei][:].rearrange("p (r n) -> p r n", r=2)[:, :, 0:N]
                act(ev, sv, AF.Exp, [t_S[si]], [t_E[ei]], scale=0.125)
                pend.append((kbg, ei, qb_lo, qb_hi))
                if len(pend) > PDEPTH:
                    stage2(*pend.pop(0))
            for p in pend:
                stage2(*p)
            rr = acc_out(use_dve=False)
            dve_ts(rd[rr][:], rd[rr][:], esink[:, i:i + 1], None, ALU.add, None, [t_rd[rr], t_esink], [t_rd[rr]])
            dve_rcp(rd[rr][:], rd[rr][:], [t_rd[rr]], [t_rd[rr]])
            dve_tt(df[:], t1[rr][:], rd[rr][:], ALU.mult, [t_t1[rr], t_rd[rr]], [t_df])
            dve_tt(yT[:, 4 + i, :], df[:], yT[:, 4 + i, :], ALU.mult, [t_df, t_yT[4 + i]], [t_yT[4 + i]])
        for g in range(2):
            for r in range(2):
                rw = slice(r * 64, r * 64 + 64)
                dve_cp(kBz[g][r][rw, 0:128], kBz[g][r][rw, 512:640], [t_kB], [t_kB])

    sq4 = [sq[0], sq[1], E[0], E[1]]
    t_sq4 = [t_sq[0], t_sq[1], t_E[0], t_E[1]]

    def c_norm_s1():
        for k in range(4):
            act(sq4[k], cf[:, k, :], AF.Square, [t_cf], [t_sq4[k]])

    def c_norm_s2(l, col):
        flush_rope()
        for k in range(4):
            mm(G[2][:], ones, sq4[k], k == 0, k == 3, [t_cmb, t_sq4[k]], [t_G[2]])
        act(nrm[:], G[2][:], AF.Sqrt, [t_G[2]], [t_nrm], scale=1.0 / 512, bias=1e-6)
        dve_rcp(nrm[:], nrm[:], [t_nrm], [t_nrm])
        for k in range(4):
            dve_stt(cn[:, k, :], cf[:, k, :], cst[:, col + k:col + k + 1], nrm[:], ALU.mult, ALU.mult, [t_cf, t_cst, t_nrm], [t_cn])

    def phaseC_in(l, c):
        tok0 = c * T
        last = (c == nb - 1)
        for i_ in range(2):
            sc.op("pool", lambda e, i_=i_: e.memset(Q[64:128, 4 + 2 * i_, :], 0.0), reads=[], writes=[t_Q[4 + 2 * i_]])
            sc.op("pool", lambda e, i_=i_: e.memset(Q[0:64, 5 + 2 * i_, :], 0.0), reads=[], writes=[t_Q[5 + 2 * i_]])
        for i in range(2):
            wsl, tw = w_next()
            for h in range(2):
                gi = fm_group(wsl, tw, h * 128)
                act(cf[:, i * 2 + h, :], G[gi][:], AF.Copy, [t_G[gi]], [t_cf])
        c_norm_s1()
        wsl, tw = w_next()
        gi = fm_group(wsl, tw, 0)
        rope_evac(gi, krCc[:], [t_krCc])
        c_norm_s2(l, C_QNW + l * 4)
        for i in range(2):
            wsl, tw = w_next()
            for h in range(2):
                gi = fm_group(wsl, tw, h * 128)
                act(yT[:, 12 + i * 2 + h, :], G[gi][:], AF.Silu, [t_G[gi]], [t_yT[12 + i * 2 + h]])
        flush_rope()
        for i in range(2):
            wsl, tw = w_next()
            for h in range(2):
                gi = fm_group(wsl, tw, h * 128)
                act(cf[:, i * 2 + h, :], G[gi][:], AF.Copy, [t_G[gi]], [t_cf])
        c_norm_s1()
        for j in range(4):
            gi = nxt("g", 2)
            for k in range(4):
                mm(G[gi][:], wuqn[:, k, j, :], cn[:, k, :], k == 0, k == 3, [t_wuqn, t_cn], [t_G[gi]])
            act(Q[:, j, :], G[gi][:], AF.Copy, [t_G[gi]], [t_Q[j]])
        for i in range(2):
            gi = nxt("g", 2)
            for k in range(4):
                mm(G[gi][:], wuqr[:, k, 2 * i:2 * i + 2, :], cn[:, k, :], k == 0, k == 3, [t_wuqr, t_cn], [t_G[gi]])
            flush_rope()
            rope_evac(gi, [(slice(0, 64), Q[0:64, 4 + 2 * i, :]), (slice(64, 128), Q[64:128, 5 + 2 * i, :])],
                      [t_Q[4 + 2 * i], t_Q[5 + 2 * i]])
        c_norm_s2(l, C_KVNW + l * 4)
        for j in range(4):
            gi = nxt("g", 2)
            for k in range(4):
                mm(G[gi][:], wukk[:, k, j, :], cn[:, k, :], k == 0, k == 3, [t_wukk, t_cn], [t_G[gi]])
            act(knCc[:, j, :], G[gi][:], AF.Copy, [t_G[gi]], [t_knCc[j]])
        tm_group(None, t_wukv, 512, lambda s: vCc[:, s, :], [t_vCc], src=cn, tsrc=t_cn, nk=4, wfn=lambda k: wukv[:, k, :, :])
        if not last:
            dma_sp(knC_d[l].rearrange("j p t -> p j t")[:, :, tok0:tok0 + T], knCc[:], t_knCc[0], t_knCc, [t_kv[l][c]["knC"]])
            dma_sp(krC_d[l][:, tok0:tok0 + T], krCc[:], t_krCc, [t_krCc], [t_kv[l][c]["krC"]])
            dma_sp(vC_d[l][tok0:tok0 + T, :].rearrange("(s p) d -> p s d", p=128), vCc[:], t_vCc, [t_vCc], [t_kv[l][c]["vC"]])
        flush_rope()

    def phaseC_attn(l, c, bgl):
        bgl = list(bgl)
        sc.marks.append(("Ca", l, c, len(sc.streams["pe"])))
        npk = c * 4
        if c > 0:
            dma_sp(krpre[:, 0:c * T], krC_d[l][:, 0:c * T], t_krpre, [t_kv[l][cc]["krC"] for cc in range(c)], [t_krpre])
        for j in range(4):
            sl = j % 2
            rows = slice((j % 2) * 64, (j % 2) * 64 + 64)
            if c > 0:
                dma_sp(kpre[sl][:, 0:c * T], knC_d[l][j, :, 0:c * T], t_kpre[sl], [t_kv[l][cc]["knC"] for cc in range(c)], [t_kpre[sl]])
                dma_sp(vpre[sl][:, 0:npk, :], vC_d[l][0:c * T, j * 128:(j + 1) * 128].rearrange("(kb p) d -> p kb d", p=128),
                       t_vpre[sl], [t_kv[l][cc]["vC"] for cc in range(c)], [t_vpre[sl]])

            def kparts(kb, j=j, sl=sl, rows=rows):
                if kb < npk:
                    a = (kpre[sl][:, kb * 128:(kb + 1) * 128], [t_kpre[sl]])
                    b = (krpre[:, kb * 128:(kb + 1) * 128], [t_krpre])
                else:
                    kk = kb - npk
                    a = (knCc[:, j, kk * 128:(kk + 1) * 128], [t_knCc[j]])
                    b = (krCc[:, kk * 128:(kk + 1) * 128], [t_krCc])
                return [(a[0], lambda q0: Q[:, j, q0:T], a[1] + [t_Q[j]]),
                        (b[0], lambda q0: Q[:, 4 + j, q0:T], b[1] + [t_Q[4 + j]])]

            def vfn(kb, j=j, sl=sl):
                if kb < npk:
                    return vpre[sl][:, kb, :], [t_vpre[sl]]
                return vCc[:, kb - npk, j * 128:(j + 1) * 128], [t_vCc]

            def fin(j=j):
                r = acc_out()
                dve_rcp(rd[r][:], rd[r][:], [t_rd[r]], [t_rd[r]])
                dve_tt(df[:], t1[r][:], rd[r][:], ALU.mult, [t_t1[r], t_rd[r]], [t_df])
                dve_tt(yT[:, 12 + j, :], df[:], yT[:, 12 + j, :], ALU.mult, [t_df, t_yT[12 + j]], [t_yT[12 + j]])

            attn_head(kparts, vfn, 192.0 ** -0.5, c, npk, fin, bg=(bgl.pop(0) if bgl else None))
        while bgl:
            bgl.pop(0)()

    t_ost = [Tl("ost%d" % i) for i in range(NSB)]
    sto = sb("sto", [128, 64], F32); t_sto = Tl("sto")

    def phaseO(l, c):
        tok0 = c * T
        dst = out_d if l == depth - 1 else x1_d
        src = x_d if l == 0 else x1_d

        xparts = [None, None,
                  [(t1[0], t_t1[0]), (t1[1], t_t1[1]), (t2[0], t_t2[0]), (t2[1], t_t2[1])],
                  [(rd[0], t_rd[0]), (rd[1], t_rd[1]), (o1, t_o1), (df, t_df)]]

        def xload(s):
            r0 = tok0 + s * 128
            if xparts[s] is None:
                dma_sp(XR[s % 2][:], src[r0:r0 + 128, :], t_XR[s % 2], [t_x1[c]] if l > 0 else [], [t_XR[s % 2]])
            else:
                for q_, (tl_ap, tl_t) in enumerate(xparts[s]):
                    dma_sp(tl_ap[:], src[r0:r0 + 128, q_ * 512:(q_ + 1) * 512], tl_t, [t_x1[c]] if l > 0 else [], [tl_t])

        for s_ in range(NSB):
            xload(s_)
        for n in range(4):
            i0 = w_next_pair()
            s0 = wslot[i0]
            assert s0 % 2 == 0 and wslot[i0 + 1] == s0 + 1
            tws = [t_W[s0], t_W[s0 + 1]]
            for s in range(NSB):
                gi = nxt("g", 2)
                for k in range(KC):
                    mm(G[gi][:, :], yT[:, k, s * 128:(s + 1) * 128], WALL[:, s0:s0 + 2, k, :], k == 0, k == KC - 1,
                       tws + [t_yT[k]], [t_G[gi]])
                obc = OB[:, s, n * 512:(n + 1) * 512]
                act(obc, G[gi][:, :], AF.Copy, [t_G[gi]], t_OB[s])
                sc.op("act", lambda e, obc=obc, s=s, n=n: e.activation(out=hb[:, 0:512], in_=obc, func=AF.Square,
                                                                      accum_out=sto[:, s * 8 + n:s * 8 + n + 1]),
                      reads=t_OB[s], writes=[t_hb, t_sto])
                dve_tt(obc, obc, postw[:, n * 512:(n + 1) * 512], ALU.mult, t_OB[s] + [t_postw], t_OB[s])
        for s in range(NSB):
            r0 = tok0 + s * 128
            xr, txr = XR[s % 2], t_XR[s % 2]
            sc.op("dve", lambda e, s=s: e.reduce_sum(out=sto[:, 32 + s:33 + s], in_=sto[:, s * 8:s * 8 + 4], axis=mybir.AxisListType.X),
                  reads=[t_sto], writes=[t_sto])
            act(sto[:, 36 + s:37 + s], sto[:, 32 + s:33 + s], AF.Sqrt, [t_sto], [t_sto], scale=1.0 / D, bias=1e-6)
            dve_rcp(sto[:, 40 + s:41 + s], sto[:, 36 + s:37 + s], [t_sto], [t_sto])
            if xparts[s] is None:
                dve_stt(OB[:, s, :], OB[:, s, :], sto[:, 40 + s:41 + s], xr[:], ALU.mult, ALU.add, t_OB[s] + [t_sto, txr], t_OB[s])
            else:
                for q_, (tl_ap, tl_t) in enumerate(xparts[s]):
                    obq = OB[:, s, q_ * 512:(q_ + 1) * 512]
                    dve_stt(obq, obq, sto[:, 40 + s:41 + s], tl_ap[:], ALU.mult, ALU.add, t_OB[s] + [t_sto, tl_t], t_OB[s])
            dma_sp(dst[r0:r0 + 128, :], OB[:, s, :], t_ost[s], t_OB[s], [t_x1[c]] if l < depth - 1 else [], final=(l == depth - 1))

    order = [(l, c) for l in range(depth) for c in range(nb)]
    phase0(0, 0)
    for idx, (l, c) in enumerate(order):
        if c == 0:
            layer_setup(l)
        sc.marks.append(("A", l, c, len(sc.streams["pe"])))
        phaseA(l, c)
        sc.marks.append(("B", l, c, len(sc.streams["pe"])))
        phaseB(l, c)
        sc.marks.append(("C", l, c, len(sc.streams["pe"])))
        phaseC_in(l, c)
        bgl = phase0_bg(*order[idx + 1]) if idx + 1 < len(order) else []
        phaseC_attn(l, c, bgl)
        sc.marks.append(("O", l, c, len(sc.streams["pe"])))
        phaseO(l, c)

    sc.finalize()
    if os.environ.get("KVERB"):
        print("NOPS", sc.nops, {e: len(v) for e, v in sc.streams.items()})
    if os.environ.get("KMARKS"):
        import json as _json
        _json.dump(sc.marks, open(os.environ["KMARKS"], "w"))
    with nc.Block() as block:
        @block.tensor
        def _(e):
            sc.emit("pe", e)

        @block.scalar
        def _(e):
            sc.emit("act", e)

        @block.vector
        def _(e):
            sc.emit("dve", e)

        @block.gpsimd
        def _(e):
            sc.emit("pool", e)

        @block.sync
        def _(e):
            sc.emit("sp", e)
    es.close()
    return nc


def host_consts(inp):
    cst = np.zeros((128, NCST), np.float32)
    p = np.arange(128)
    for l in range(DEPTH):
        cst[:, C_PREW + l * 16:C_PREW + (l + 1) * 16] = np.asarray(inp["pre_norm_w"][l]).reshape(16, 128).T
        cst[:, C_QNW + l * 4:C_QNW + (l + 1) * 4] = np.asarray(inp["mla_q_norm_w"][l]).reshape(4, 128).T
        cst[:, C_KVNW + l * 4:C_KVNW + (l + 1) * 4] = np.asarray(inp["mla_kv_norm_w"][l]).reshape(4, 128).T
        cst[:, C_SUBLN + l] = np.asarray(inp["diff_subln_w"][l])
        sk = np.asarray(inp["sink_logits"][l])
        for i in range(8):
            cst[:, C_SINK + l * 8 + i] = sk[2 * i + p // 64]
        for v, nm in enumerate(("diff_lambda_q1", "diff_lambda_k1", "diff_lambda_q2", "diff_lambda_k2")):
            cst[:, C_LAM + l * 256 + v * 64:C_LAM + l * 256 + (v + 1) * 64] = np.asarray(inp[nm][l])[None, :]
    cst[:, C_INVF] = (10000.0 ** (-((p % 32) * 2) / 64.0)).astype(np.float32)
    cmat = np.zeros((128, 5, 128), np.float32)
    cmat[:, 0, :] = np.eye(128)
    R = np.zeros((128, 128), np.float32)
    for f in range(128):
        if f % 64 < 32:
            R[f, f + 32] = -1.0
        else:
            R[f, f - 32] = 1.0
    cmat[:, 1, :] = R.T
    k = np.arange(128)[:, None]
    q = np.arange(128)[None, :]
    cmat[:, 2, :] = (q >= k)
    cmat[:, 3, :] = (k > q)
    cmat[:, 4, :] = 1.0
    postw = np.ascontiguousarray(np.broadcast_to(np.asarray(inp["post_norm_w"])[:, None, :], (DEPTH, 128, D))).astype(np.float32)
    return cst, cmat, postw


_NC_CACHE = {}


def kernel(**inp):
    x = np.asarray(inp["x"], np.float32)
    pos = np.asarray(inp["positions"], np.int32)
    B = x.shape[0]
    cst, cmat, postw = host_consts(inp)
    w_in = np.ascontiguousarray(np.asarray(inp["w_in"], np.float32))
    w_out = np.ascontiguousarray(np.asarray(inp["w_out"], np.float32))
    w_uq = np.ascontiguousarray(np.asarray(inp["w_uq"], np.float32))
    w_ukv = np.ascontiguousarray(np.asarray(inp["w_ukv"], np.float32))
    if "nc" not in _NC_CACHE:
        _NC_CACHE["nc"] = build_nc()
    nc = _NC_CACHE["nc"]
    in_maps = []
    for core in range(NCORES):
        b = core % B
        in_maps.append({
            "x": np.ascontiguousarray(x[b]),
            "pos": np.ascontiguousarray(np.broadcast_to(pos[b][None, :], (128, S))),
            "cst": cst, "cmat": cmat, "postw": postw,
            "w_in": w_in, "w_out": w_out, "w_uq": w_uq, "w_ukv": w_ukv,
        })
    res = run_bass_kernel_spmd(nc, in_maps, core_ids=list(range(NCORES)))
    out = np.stack([np.asarray(res.results[b]["out"], np.float32) for b in range(B)], axis=0)
    return out
```
